# Optimizing a Trainium2 kernel written in Bass

```python
import jax, jax.numpy as jnp
from jax import lax
import numpy as np

D_MODEL = 1024
BATCH = 8
SEQ = 4096
DEPTH = 4

CHUNK = 64
N_MIXERS = 2
EXPAND = 2
BRANCH = EXPAND * D_MODEL
GMLP_BLOCK = 128
A_GROUPS = 8
A_GROUP_DIM = BRANCH // A_GROUPS
POOL_WINDOWS = (2, 4, 8, 16)
B_GROUPS = len(POOL_WINDOWS)
B_GROUP_DIM = BRANCH // B_GROUPS
N_A = (DEPTH + 1) // 2
N_B = DEPTH // 2
EPS = 1e-6

kernel_name = "hybrid_gmlp_pool_sandwich_trunk"


def rms_norm(x, g):
    xf = x.astype(jnp.float32)
    y = xf * lax.rsqrt(jnp.mean(xf * xf, axis=-1, keepdims=True) + EPS)
    return (y * g.astype(jnp.float32)).astype(x.dtype)


def layer_norm(x, g, b):
    xf = x.astype(jnp.float32)
    mu = jnp.mean(xf, axis=-1, keepdims=True)
    xc = xf - mu
    y = xc * lax.rsqrt(jnp.mean(xc * xc, axis=-1, keepdims=True) + EPS)
    return (y * g.astype(jnp.float32) + b.astype(jnp.float32)).astype(x.dtype)


def spatial_mask():
    p = jnp.arange(GMLP_BLOCK)
    return (p[None, :] // CHUNK) <= (p[:, None] // CHUNK)


def gmlp_mixer(h, w_in, ln_g, ln_b, w_s, b_s, w_out):
    B, S, _ = h.shape
    proj = h @ w_in
    u, v, z = jnp.split(proj, 3, axis=-1)
    u = jax.nn.gelu(u)
    v = layer_norm(jax.nn.gelu(v), ln_g, ln_b)
    vb = v.reshape(B, S // GMLP_BLOCK, GMLP_BLOCK, A_GROUPS, A_GROUP_DIM)
    w = jnp.where(spatial_mask()[None], w_s, jnp.zeros_like(w_s))
    mixed = jnp.einsum('gpq,bnqgc->bnpgc', w, vb)
    mixed = mixed + jnp.transpose(b_s)[None, None, :, :, None]
    mixed = mixed.reshape(B, S, BRANCH)
    y = u * mixed * jax.nn.silu(z)
    return y @ w_out


def pool_mixer(h, w_in, w_grp, scale, w_out):
    B, S, _ = h.shape
    proj = h @ w_in
    xb, z = jnp.split(proj, 2, axis=-1)
    xf = xb.astype(jnp.float32)
    cs = jnp.concatenate([jnp.zeros((B, 1, BRANCH), jnp.float32),
                          jnp.cumsum(xf, axis=1)], axis=1)
    upper = cs[:, 1:]
    t1 = jnp.arange(1, S + 1, dtype=jnp.int32)
    outs = []
    for gi, win in enumerate(POOL_WINDOWS):
        sl = slice(gi * B_GROUP_DIM, (gi + 1) * B_GROUP_DIM)
        lower = jnp.pad(cs[:, :S + 1 - win, sl], ((0, 0), (win - 1, 0), (0, 0)))
        count = jnp.minimum(t1, win).astype(jnp.float32)[None, :, None]
        pooled = (upper[:, :, sl] - lower) / count - xf[:, :, sl]
        outs.append(jnp.einsum('bsc,cd->bsd', pooled.astype(xb.dtype), w_grp[gi]))
    mixed = jnp.concatenate(outs, axis=-1) * scale
    y = mixed * jax.nn.silu(z)
    return y @ w_out


def setup_inputs(seed: int = 0) -> dict:
    key = jax.random.key(seed)
    ks = jax.random.split(key, 16)
    f32 = jnp.float32
    nrm = lambda k, shape, s: jax.random.normal(k, shape, f32) * s
    return {
        "x": nrm(ks[0], (BATCH, SEQ, D_MODEL), 1.0),
        "norm_pre": 1.0 + nrm(ks[1], (DEPTH, D_MODEL), 0.05),
        "norm_post": 1.0 + nrm(ks[2], (DEPTH, D_MODEL), 0.05),
        "a_w_in": nrm(ks[3], (N_A, D_MODEL, 3 * BRANCH), D_MODEL ** -0.5),
        "a_ln_g": 1.0 + nrm(ks[4], (N_A, BRANCH), 0.05),
        "a_ln_b": nrm(ks[5], (N_A, BRANCH), 0.02),
        "a_w_s": nrm(ks[6], (N_A, A_GROUPS, GMLP_BLOCK, GMLP_BLOCK), GMLP_BLOCK ** -0.5),
        "a_b_s": 1.0 + nrm(ks[7], (N_A, A_GROUPS, GMLP_BLOCK), 0.05),
        "a_w_out": nrm(ks[8], (N_A, BRANCH, D_MODEL), BRANCH ** -0.5),
        "b_w_in": nrm(ks[9], (N_B, D_MODEL, 2 * BRANCH), D_MODEL ** -0.5),
        "b_w_grp": nrm(ks[10], (N_B, B_GROUPS, B_GROUP_DIM, B_GROUP_DIM), B_GROUP_DIM ** -0.5),
        "b_scale": 1.0 + nrm(ks[11], (N_B, BRANCH), 0.1),
        "b_w_out": nrm(ks[12], (N_B, BRANCH, D_MODEL), BRANCH ** -0.5),
    }


def reference(x, norm_pre, norm_post, a_w_in, a_ln_g, a_ln_b, a_w_s, a_b_s, a_w_out,
              b_w_in, b_w_grp, b_scale, b_w_out):
    for i in range(DEPTH):
        h = rms_norm(x, norm_pre[i])
        j = i // N_MIXERS
        if i % N_MIXERS == 0:
            out = gmlp_mixer(h, a_w_in[j], a_ln_g[j], a_ln_b[j], a_w_s[j], a_b_s[j], a_w_out[j])
        else:
            out = pool_mixer(h, b_w_in[j], b_w_grp[j], b_scale[j], b_w_out[j])
        x = x + rms_norm(out, norm_post[i])
    return x
```

```python
from contextlib import ExitStack

import numpy as np
import concourse.bass as bass
import concourse.mybir as mybir
from concourse.bass_utils import run_bass_kernel_spmd

F32 = mybir.dt.float32
BF16 = mybir.dt.bfloat16
AF = mybir.ActivationFunctionType
ALU = mybir.AluOpType

D = 1024
E = 2048
T = 512
NB = 4
DEPTH = 4
EPS = 1e-6
NSLOT = 5
WINDOWS = (2, 4, 8, 16)
NPA = 16
NPB = 16


class Region:
    __slots__ = ("name", "lw", "rd", "const")

    def __init__(self, name, const=False):
        self.name = name
        self.lw = None
        self.rd = []
        self.const = const


class Trk:
    def __init__(self, nc, es):
        self.nc = nc
        self.es = es
        self.eng = {"pe": nc.tensor, "act": nc.scalar, "dve": nc.vector, "pool": nc.gpsimd, "sp": nc.sync}
        self.sems = {}
        self.cnt = {}
        self.waited = {e: {} for e in self.eng}
        self.pend = {e: ([], []) for e in self.eng}
        self.esem = {}
        for e in ("pe", "act", "dve", "pool"):
            self.esem[e] = self.sem("c_" + e)

    def sem(self, name):
        if name not in self.sems:
            self.sems[name] = self.es.enter_context(self.nc.semaphore(name))
            self.cnt[name] = 0
        return name

    def op(self, e, fn, reads=(), writes=(), sig=True, sem=None, inc=1):
        deps = {}

        def add(ev):
            if ev is None:
                return
            s, v = ev
            if deps.get(s, 0) < v:
                deps[s] = v

        for r in reads:
            add(r.lw)
        for w in writes:
            add(w.lw)
            for ev in w.rd:
                add(ev)
        eng = self.eng[e]
        wd = self.waited[e]
        for s, v in deps.items():
            if wd.get(s, 0) < v:
                eng.wait_ge(self.sems[s], v)
                wd[s] = v
        inst = fn(eng)
        pr, pw = self.pend[e]
        pr.extend(r for r in reads if not r.const)
        pw.extend(writes)
        if sig:
            s = sem or self.esem[e]
            self.cnt[s] += inc
            inst.then_inc(self.sems[s], inc)
            ev = (s, self.cnt[s])
            for r in pr:
                r.rd.append(ev)
            for w in pw:
                w.lw = ev
                w.rd = []
            pr.clear()
            pw.clear()
        return inst

    def barrier(self, engines=("pe", "act", "dve", "pool", "sp")):
        for e in engines:
            for s, c in self.cnt.items():
                if c > 0 and self.waited[e].get(s, 0) < c:
                    self.eng[e].wait_ge(self.sems[s], c)
                    self.waited[e][s] = c


def handoff(group, new):
    evs = []
    for r in group:
        if r.lw is not None:
            evs.append(r.lw)
        evs.extend(r.rd)
    for r in new:
        r.rd.extend(evs)


def build_program(n_tiles=8, layers=(0, 1, 2, 3), dbg=False):
    ntok = n_tiles * T
    nc = bass.Bass("TRN2", target_bir_lowering=False)
    dt_in = lambda name, shape: nc.dram_tensor(name, shape, F32, kind="ExternalInput").ap()
    x_d = dt_in("x", [ntok, D])
    npre_d = dt_in("norm_pre", [DEPTH, D])
    npost_d = dt_in("norm_post", [DEPTH, D])
    a_w_in = dt_in("a_w_in", [2, D, 3 * E])
    a_ln_g = dt_in("a_ln_g", [2, 128, 16])
    a_ln_b = dt_in("a_ln_b", [2, 128, 16])
    a_w_s = dt_in("a_w_s", [2, 128, 8, 128])
    a_b_s = dt_in("a_b_s", [2, 1024])
    a_w_out = dt_in("a_w_out", [2, E, D])
    b_w_in = dt_in("b_w_in", [2, D, 2 * E])
    b_w_grp = dt_in("b_w_grp", [2, 4, 512, 512])
    b_scale = dt_in("b_scale", [2, 128, 16])
    b_w_out = dt_in("b_w_out", [2, E, D])
    out_d = nc.dram_tensor("out", [ntok, D], F32, kind="ExternalOutput").ap()
    scr = nc.dram_tensor("scr", [2 * NPA + 2 * NPB, 128, 4096], BF16, kind="Internal").ap()
    if dbg:
        d_hT = nc.dram_tensor("d_hT", [128, 8, T], BF16, kind="ExternalOutput").ap()
        d_vln = nc.dram_tensor("d_vln", [128, NB, E], BF16, kind="ExternalOutput").ap()
        d_yT = nc.dram_tensor("d_yT", [128, 16, T], BF16, kind="ExternalOutput").ap()
        d_ob = nc.dram_tensor("d_ob", [128, NB, D], F32, kind="ExternalOutput").ap()
        d_pl = nc.dram_tensor("d_pl", [128, 16, T], BF16, kind="ExternalOutput").ap()
        d_B = nc.dram_tensor("d_B", [128, 2, 16, 128], F32, kind="ExternalOutput").ap()
        d_tmp = nc.dram_tensor("d_tmp", [128, 2, 3, T], F32, kind="ExternalOutput").ap()
        d_g = nc.dram_tensor("d_g", [128, 2, 16], F32, kind="ExternalOutput").ap()
        d_ws = nc.dram_tensor("d_ws", [128, 2, 8, 128], BF16, kind="ExternalOutput").ap()

    def a_pieces(j):
        win = a_w_in[j].rearrange("(k p) c -> p k c", p=128)
        wout = a_w_out[j].rearrange("(k p) c -> p k c", p=128)
        P = []
        for cg in range(4):
            P.append((win[:, :, E + cg * 512:E + (cg + 1) * 512], 8))
        for jj in range(4):
            P.append((win[:, :, jj * 512:(jj + 1) * 512], 8))
            P.append((win[:, :, 2 * E + jj * 512:2 * E + (jj + 1) * 512], 8))
        for kh in range(2):
            for ch in range(2):
                P.append((wout[:, kh * 8:(kh + 1) * 8, ch * 512:(ch + 1) * 512], 8))
        return P

    def b_pieces(j):
        win = b_w_in[j].rearrange("(k p) c -> p k c", p=128)
        wg = b_w_grp[j].rearrange("g (k p) c -> p g k c", p=128)
        wout = b_w_out[j].rearrange("(k p) c -> p k c", p=128)
        P = [(win[:, :, jj * 512:(jj + 1) * 512], 8) for jj in range(4)]
        for gi in range(4):
            P.append((wg[:, gi], 4))
            P.append((win[:, :, E + gi * 512:E + (gi + 1) * 512], 8))
        for kh in range(2):
            for ch in range(2):
                P.append((wout[:, kh * 8:(kh + 1) * 8, ch * 512:(ch + 1) * 512], 8))
        return P

    layer_base = {}
    layer_pieces = {}
    base = 0
    for l in range(DEPTH):
        layer_base[l] = base
        layer_pieces[l] = a_pieces(l // 2) if l % 2 == 0 else b_pieces(l // 2)
        base += len(layer_pieces[l])

    es = ExitStack()
    with es:
        trk = Trk(nc, es)
        sb = lambda name, shape, dt: es.enter_context(nc.sbuf_tensor(name, shape, dt))

        ident = sb("ident", [128, 128], BF16)
        eps_t = sb("eps_t", [128, 1], F32)
        wsT = sb("wsT", [128, 2, 8, 128], BF16)
        Bhalf = sb("Bhalf", [128, 2, 16, 128], F32)
        ghalf = sb("ghalf", [128, 2, 16], F32)
        bscale = sb("bscale", [128, 2, 16], F32)
        invc = sb("invc", [128, 4, 16], F32)
        carry = sb("carry", [128, 2, 16, 16], F32)
        R_const = Region("const", const=True)
        R_carry = [[Region(f"carry{lb}_{c}") for c in range(16)] for lb in range(2)]

        R_scr = {}
        xres = sb("xres", [128, 2, NB, D], F32)
        hT = sb("hT", [128, 8, T], BF16)
        h_tm = sb("h_tm", [128, 3, D], BF16)
        big = sb("big", [128, 8192], F32)
        vln = sb("vln", [128, NB, E], BF16)
        yT = sb("yT", [128, 16, T], BF16)
        wring = sb("wring", [128, NSLOT, 8, 512], BF16)
        tmpA = sb("tmpA", [128, 8, T], F32)
        gains = sb("gains", [128, 2, 2, D], F32)
        junk = sb("junk", [128, D], BF16)
        stats = sb("stats", [128, NB, 4, 6], F32)
        mv = sb("mv", [128, NB, 2], F32)
        nrm = sb("nrm", [128, 8, NB], F32)
        t16 = sb("t16", [128, 2, 16], F32)
        sso = sb("sso", [128, 2, NB], F32)
        banks = [es.enter_context(nc.psum_tensor(f"bank{i}", [128, 512], F32)) for i in range(8)]

        vbuf = big[:].rearrange("p (b e) -> p b e", b=NB)
        obuf = big[:, 0:4096].rearrange("p (b d) -> p b d", b=NB)
        pooledT = big[:, 0:4096].bitcast(BF16).rearrange("p (c t) -> p c t", c=16)
        PBW = 528
        pbv = big[:, 4096:4096 + 7 * PBW].rearrange("p (i w) -> p i w", i=7)

        R_x = [[Region(f"x{u}_{b}") for b in range(NB)] for u in range(2)]
        R_hT = [Region(f"hT{b}") for b in range(NB)]
        R_htm = [Region(f"htm{i}") for i in range(3)]
        R_vbuf = [[Region(f"vbuf{b}_{cg}") for cg in range(4)] for b in range(NB)]
        R_obuf = [[Region(f"obuf{b}_{ch}") for ch in range(2)] for b in range(NB)]
        R_pooled = [Region(f"pooled{c}") for c in range(16)]
        R_pb = [[Region(f"pb{s}_{i}") for i in range(2)] for s in range(3)] + \
               [[Region(f"pbt{e}_{i}") for i in range(2)] for e in range(2)]
        big_group = [r for row in R_vbuf for r in row] + [r for row in R_obuf for r in row] + R_pooled + \
                    [r for row in R_pb for r in row]
        R_vln = [Region(f"vln{b}") for b in range(NB)]
        R_yT = [Region(f"yT{c}") for c in range(16)]
        R_slot = [Region(f"slot{s}") for s in range(NSLOT)]
        R_tmp = [Region(f"tmp{i}") for i in range(8)]
        R_gain = [[Region(f"gain{u}_{i}") for i in range(2)] for u in range(2)]
        R_junk = Region("junk")
        R_stats = [[Region(f"st{b}_{cg}") for cg in range(4)] for b in range(NB)]
        R_mv = [Region(f"mv{b}") for b in range(NB)]
        R_nrm = [[Region(f"nrm{i}_{b}") for b in range(NB)] for i in range(8)]
        R_t16 = [Region("t16_0"), Region("t16_1")]
        R_sso = [[Region(f"sso{ch}_{b}") for b in range(NB)] for ch in range(2)]
        R_bank = [Region(f"bank{i}") for i in range(8)]
        for s in range(NSLOT):
            trk.sem(f"w{s}")
        for u in range(2):
            trk.sem(f"xl{u}")
            trk.sem(f"xs{u}")
            trk.sem(f"gl{u}0")
            trk.sem(f"gl{u}1")

        bank_free = list(range(8))

        def next_bank():
            assert bank_free, "out of PSUM banks"
            return bank_free.pop(0)

        def free_bank(i):
            assert i not in bank_free
            bank_free.append(i)

        wseq = []
        for ti in range(n_tiles):
            for l in layers:
                for i in range(len(layer_pieces[l])):
                    wseq.append((ti, layer_base[l] + i, layer_pieces[l][i][1], layer_pieces[l][i][0]))
        wst = {"load": 0, "use": 0, "done": 0}
        R_scrp = {}
        for s_ in range(NSLOT):
            trk.sem(f"wb{s_}")

        def w_prefetch():
            while wst["load"] < len(wseq) and wst["load"] < wst["done"] + NSLOT:
                i = wst["load"]
                s = i % NSLOT
                ti_, si, nk, src = wseq[i]
                if ti_ == 0:
                    trk.op("pool", lambda g, s=s, nk=nk, src=src: g.dma_start(out=wring[:, s, 0:nk], in_=src),
                           writes=[R_slot[s]], sem=f"w{s}", inc=16)
                    if n_tiles > 1:
                        R_scrp[si] = Region(f"scrp{si}")
                        trk.op("sp", lambda g, s=s, si=si, nk=nk: g.dma_start(
                            out=scr[si][:, 0:nk * 512], in_=wring[:, s, 0:nk].rearrange("p k c -> p (k c)")),
                            reads=[R_slot[s]], writes=[R_scrp[si]], sem=f"wb{s}", inc=16)
                else:
                    trk.op("sp", lambda g, s=s, si=si, nk=nk: g.dma_start(
                        out=wring[:, s, 0:nk].rearrange("p k c -> p (k c)"), in_=scr[si][:, 0:nk * 512]),
                        reads=[R_scrp[si]], writes=[R_slot[s]], sem=f"w{s}", inc=16)
                wst["load"] += 1

        def w_get():
            w_prefetch()
            assert wst["load"] > wst["use"], "weight ring deadlock"
            s = wst["use"] % NSLOT
            wst["use"] += 1
            return s

        def w_done(n=1):
            wst["done"] += n
            w_prefetch()

        def rsqrt_into(dst_ap, dst_reg, src_ap, src_reg, scale):
            trk.op("act", lambda g: g.activation(out=dst_ap, in_=src_ap, func=AF.Sqrt, bias=eps_t[:, 0:1], scale=scale),
                   reads=[src_reg, R_const], writes=[dst_reg])
            trk.op("dve", lambda g: g.reciprocal(out=dst_ap, in_=dst_ap), reads=[dst_reg], writes=[dst_reg])

        def load_gain(l, which):
            u = l % 2
            src = npre_d if which == 0 else npost_d
            trk.op("sp", lambda g: g.dma_start(out=gains[:, u, which, :], in_=src[l].partition_broadcast(128)),
                   writes=[R_gain[u][which]], sem=f"gl{u}{which}", inc=16)

        def rsqrt_cols(dst_row, src_row, b0, b1, scale):
            regs_s = [R_nrm[src_row][b] for b in range(b0, b1)]
            regs_d = [R_nrm[dst_row][b] for b in range(b0, b1)]
            trk.op("act", lambda g: g.activation(out=nrm[:, dst_row, b0:b1], in_=nrm[:, src_row, b0:b1], func=AF.Sqrt,
                                                 bias=eps_t[:, 0:1], scale=scale),
                   reads=regs_s + [R_const], writes=regs_d)
            trk.op("dve", lambda g: g.reciprocal(out=nrm[:, dst_row, b0:b1], in_=nrm[:, dst_row, b0:b1]),
                   reads=regs_d, writes=regs_d)

        pending_late = []

        def run_late():
            while pending_late:
                pending_late.pop(0)()

        def pre_A(xu, bl, l):
            u = l % 2
            for b in bl:
                xv = xres[:, xu, b, :]
                trk.op("act", lambda g, b=b, xv=xv: g.activation(out=junk[:], in_=xv, func=AF.Square, accum_out=nrm[:, 2, b:b + 1]),
                       reads=[R_x[xu][b]], writes=[R_nrm[2][b]])
            for b in bl:
                rsqrt_cols(3, 2, b, b + 1, 1.0 / D)
            for b in bl:
                hb = (0, 1, 2, 0)[b]
                xv = xres[:, xu, b, :]
                trk.op("dve", lambda g, b=b, xv=xv, hb=hb: g.scalar_tensor_tensor(out=h_tm[:, hb, :], in0=xv, scalar=nrm[:, 3, b:b + 1],
                                                                                  in1=gains[:, u, 0, :], op0=ALU.mult, op1=ALU.mult),
                       reads=[R_x[xu][b], R_nrm[3][b], R_gain[u][0]], writes=[R_htm[hb]])

        def pre_B(bl):
            for b in bl:
                hb = (0, 1, 2, 0)[b]
                bi = next_bank()
                tp = banks[bi][:].bitcast(BF16).rearrange("p (k t) -> p k t", t=128)
                for k in range(8):
                    trk.op("pe", lambda g, k=k, tp=tp, hb=hb: g.transpose(tp[:, k, :], h_tm[:, hb, k * 128:(k + 1) * 128], ident[:]),
                           reads=[R_htm[hb], R_const], writes=[R_bank[bi]], sig=(k == 7))
                if b % 2 == 0:
                    trk.op("act", lambda g, b=b, tp=tp: g.activation(out=hT[:, :, b * 128:(b + 1) * 128], in_=tp, func=AF.Copy),
                           reads=[R_bank[bi]], writes=[R_hT[b]])
                else:
                    trk.op("dve", lambda g, b=b, tp=tp: g.tensor_copy(out=hT[:, :, b * 128:(b + 1) * 128], in_=tp),
                           reads=[R_bank[bi]], writes=[R_hT[b]])
                free_bank(bi)

        def out_proj_and_norm(xu, l, yreg_all, nxt):
            u = l % 2
            handoff(big_group, [r for row in R_obuf for r in row])
            if dbg:
                trk.sem("dbg")
                trk.op("pool", lambda g: g.dma_start(out=d_yT, in_=yT[:]), reads=R_yT, sem="dbg", inc=16)
            ss = [w_get() for _ in range(4)]
            groups = [[0, 1], [2], [3]]
            bis = [[None] * NB for _ in range(2)]

            def mm(half, release):
                for ch in range(2):
                    for b in half:
                        bis[ch][b] = next_bank()
                for kh in range(2):
                    for ch in range(2):
                        s = ss[kh * 2 + ch]
                        for b in half:
                            for k in range(8):
                                first = (kh == 0 and k == 0)
                                last = (kh == 1 and k == 7)
                                trk.op("pe", lambda g, b=b, k=k, s=s, kh=kh, ch=ch, first=first, last=last: g.matmul(
                                    banks[bis[ch][b]][:], lhsT=yT[:, kh * 8 + k, b * 128:(b + 1) * 128], rhs=wring[:, s, k, :],
                                    start=first, stop=last),
                                    reads=([yreg_all[kh * 8 + kk] for kk in range(8)] + [R_slot[s]]) if k == 0 else [R_slot[s]], writes=[R_bank[bis[ch][b]]],
                                    sig=(last or (b == half[-1] and k == 7)))
                        if release:
                            w_done()

            def post(bl):
                b0, b1 = bl[0], bl[-1] + 1
                for ch in range(2):
                    for b in bl:
                        trk.op("act", lambda g, b=b, ch=ch: g.activation(out=junk[:, 0:512], in_=banks[bis[ch][b]][:], func=AF.Square,
                                                                          accum_out=sso[:, ch, b:b + 1]),
                               reads=[R_bank[bis[ch][b]]], writes=[R_sso[ch][b]])
                trk.op("dve", lambda g: g.tensor_tensor(out=nrm[:, 0, b0:b1], in0=sso[:, 0, b0:b1], in1=sso[:, 1, b0:b1], op=ALU.add),
                       reads=[R_sso[c][b] for c in range(2) for b in bl], writes=[R_nrm[0][b] for b in bl])
                rsqrt_cols(1, 0, b0, b1, 1.0 / D)
                for b in bl:
                    xv = xres[:, xu, b, :]
                    for ch in range(2):
                        trk.op("dve", lambda g, b=b, ch=ch: g.scalar_tensor_tensor(
                            out=obuf[:, b, ch * 512:(ch + 1) * 512], in0=banks[bis[ch][b]][:], scalar=nrm[:, 1, b:b + 1],
                            in1=gains[:, u, 1, ch * 512:(ch + 1) * 512], op0=ALU.mult, op1=ALU.mult),
                            reads=[R_bank[bis[ch][b]], R_nrm[1][b], R_gain[u][1]], writes=[R_obuf[b][ch]])
                    free_bank(bis[0][b])
                    free_bank(bis[1][b])
                    trk.op("dve", lambda g, b=b, xv=xv: g.tensor_tensor(out=xv, in0=obuf[:, b, :], in1=xv, op=ALU.add),
                           reads=R_obuf[b] + [R_x[xu][b]], writes=[R_x[xu][b]])

            mm(groups[0], False)
            post(groups[0])
            if nxt is not None:
                pre_A(nxt[0], groups[0], nxt[1])
            mm(groups[1], False)
            post(groups[1])
            if nxt is not None:
                pre_A(nxt[0], groups[1], nxt[1])
            mm(groups[2], True)
            if nxt is not None:
                pre_B(groups[0])
            post(groups[2])
            if nxt is not None:
                pre_A(nxt[0], groups[2], nxt[1])
                pending_late.append(lambda: (pre_B(groups[1]), pre_B(groups[2])))

        def ln_all():
            for b in range(NB):
                trk.op("dve", lambda g, b=b: g.bn_aggr(out=mv[:, b, :], in_=stats[:, b].rearrange("p c s -> p (c s)")),
                       reads=R_stats[b], writes=[R_mv[b]])
            trk.op("act", lambda g: g.activation(out=nrm[:, 4, :], in_=mv[:, :, 1], func=AF.Sqrt, bias=eps_t[:, 0:1], scale=1.0),
                   reads=R_mv + [R_const], writes=R_nrm[4])
            trk.op("dve", lambda g: g.reciprocal(out=nrm[:, 4, :], in_=nrm[:, 4, :]), reads=R_nrm[4], writes=R_nrm[4])
            trk.op("dve", lambda g: g.scalar_tensor_tensor(out=nrm[:, 5, :], in0=mv[:, :, 0], scalar=-1.0, in1=nrm[:, 4, :],
                                                           op0=ALU.mult, op1=ALU.mult),
                   reads=R_mv + R_nrm[4], writes=R_nrm[5])
            for b in range(NB):
                if b % 2 == 0:
                    trk.op("dve", lambda g, b=b: g.tensor_scalar(out=vln[:, b, :], in0=vbuf[:, b, :], scalar1=mv[:, b, 0:1],
                                                                 scalar2=nrm[:, 4, b:b + 1], op0=ALU.subtract, op1=ALU.mult),
                           reads=R_vbuf[b] + [R_mv[b], R_nrm[4][b]], writes=[R_vln[b]])
                else:
                    trk.op("act", lambda g, b=b: g.activation(out=vln[:, b, :], in_=vbuf[:, b, :], func=AF.Identity,
                                                              bias=nrm[:, 5, b:b + 1], scale=nrm[:, 4, b:b + 1]),
                           reads=R_vbuf[b] + [R_nrm[5][b], R_nrm[4][b]], writes=[R_vln[b]])

        def layer_A(xu, l, nxt):
            j = l // 2
            handoff(big_group, [r for row in R_vbuf for r in row])
            sv4 = [w_get() for _ in range(4)]

            def a1_job(cg, b):
                s = sv4[cg]
                bi = next_bank()
                for k in range(8):
                    trk.op("pe", lambda g, k=k: g.matmul(banks[bi][:], lhsT=hT[:, k, b * 128:(b + 1) * 128], rhs=wring[:, s, k, :],
                                                         start=(k == 0), stop=(k == 7)),
                           reads=[R_hT[b], R_slot[s]], writes=[R_bank[bi]], sig=(k == 7))
                vv = vbuf[:, b, cg * 512:(cg + 1) * 512]
                trk.op("act", lambda g: g.activation(out=vv, in_=banks[bi][:], func=AF.Gelu_apprx_tanh),
                       reads=[R_bank[bi]], writes=[R_vbuf[b][cg]])
                free_bank(bi)
                trk.op("dve", lambda g: g.bn_stats(out=stats[:, b, cg, :], in_=vv),
                       reads=[R_vbuf[b][cg]], writes=[R_stats[b][cg]])

            for cg in range(4):
                for b in (0, 1):
                    a1_job(cg, b)
            run_late()
            for cg in range(4):
                for b in (2, 3):
                    a1_job(cg, b)
                w_done()
            ln_all()
            if dbg:
                trk.sem("dbg")
                trk.op("pool", lambda g: g.dma_start(out=d_hT, in_=hT[:]), reads=R_hT, sem="dbg", inc=16)
                trk.op("pool", lambda g: g.dma_start(out=d_vln, in_=vln[:]), reads=R_vln, sem="dbg", inc=16)
            LAG = 3
            pend = []

            def finish(c):
                gg = c // 2
                ub = c % 4
                tb = c % 2
                bm = next_bank()
                for b in range(NB):
                    trk.op("pe", lambda g, b=b: g.matmul(banks[bm][:, b * 128:(b + 1) * 128],
                                                         lhsT=vln[:, b, c * 128:(c + 1) * 128],
                                                         rhs=wsT[:, j, gg, :], start=True, stop=True),
                           reads=[R_vln[b], R_const], writes=[R_bank[bm]], sig=(b == NB - 1))
                t1 = tmpA[:, 6 + tb, :]
                uv = tmpA[:, ub, :]
                trk.op("dve", lambda g: g.scalar_tensor_tensor(
                    out=t1.rearrange("p (b t) -> p b t", t=128),
                    in0=banks[bm][:].rearrange("p (b t) -> p b t", t=128),
                    scalar=ghalf[:, j, c:c + 1],
                    in1=Bhalf[:, j, c:c + 1, :].to_broadcast([128, NB, 128]),
                    op0=ALU.mult, op1=ALU.add),
                    reads=[R_bank[bm], R_const], writes=[R_tmp[6 + tb]])
                free_bank(bm)
                trk.op("dve", lambda g: g.tensor_tensor(out=yT[:, c, :], in0=t1, in1=uv, op=ALU.mult),
                       reads=[R_tmp[6 + tb], R_tmp[ub]], writes=[R_yT[c]])

            for jj in range(4):
                su = w_get()
                sz = w_get()
                for cc in range(4):
                    c = jj * 4 + cc
                    ub = c % 4
                    tb = c % 2
                    bu, bz = next_bank(), next_bank()
                    for k in range(8):
                        trk.op("pe", lambda g, k=k, cc=cc: g.matmul(banks[bu][:], lhsT=wring[:, su, k, cc * 128:(cc + 1) * 128],
                                                                    rhs=hT[:, k, :], start=(k == 0), stop=(k == 7)),
                               reads=R_hT + [R_slot[su]], writes=[R_bank[bu]], sig=(k == 7))
                    for k in range(8):
                        trk.op("pe", lambda g, k=k, cc=cc: g.matmul(banks[bz][:], lhsT=wring[:, sz, k, cc * 128:(cc + 1) * 128],
                                                                    rhs=hT[:, k, :], start=(k == 0), stop=(k == 7)),
                               reads=R_hT + [R_slot[sz]], writes=[R_bank[bz]], sig=(k == 7))
                    uv, sv = tmpA[:, ub, :], tmpA[:, 4 + tb, :]
                    trk.op("act", lambda g, uv=uv: g.activation(out=uv, in_=banks[bu][:], func=AF.Gelu_apprx_tanh),
                           reads=[R_bank[bu]], writes=[R_tmp[ub]])
                    free_bank(bu)
                    trk.op("act", lambda g, sv=sv: g.activation(out=sv, in_=banks[bz][:], func=AF.Tanh, scale=0.5),
                           reads=[R_bank[bz]], writes=[R_tmp[4 + tb]])
                    trk.op("dve", lambda g, sv=sv: g.scalar_tensor_tensor(out=sv, in0=sv, scalar=1.0, in1=banks[bz][:],
                                                                         op0=ALU.add, op1=ALU.mult),
                           reads=[R_tmp[4 + tb], R_bank[bz]], writes=[R_tmp[4 + tb]])
                    free_bank(bz)
                    trk.op("pool", lambda g, uv=uv, sv=sv: g.tensor_tensor(out=uv, in0=uv, in1=sv, op=ALU.mult),
                           reads=[R_tmp[ub], R_tmp[4 + tb]], writes=[R_tmp[ub]])
                    pend.append(c)
                    if len(pend) > LAG:
                        finish(pend.pop(0))
                w_done(2)
            while pend:
                finish(pend.pop(0))
            out_proj_and_norm(xu, l, R_yT, nxt)

        def layer_B(xu, l, ti, nxt):
            j = l // 2
            lb = j
            handoff(big_group, R_pooled + [r for row in R_pb for r in row])
            def b1_mm(s, cc, bi, lo, hi):
                for k in range(8):
                    trk.op("pe", lambda g, k=k: g.matmul(banks[bi][:, lo:hi], lhsT=wring[:, s, k, cc * 128:(cc + 1) * 128],
                                                         rhs=hT[:, k, lo:hi], start=(k == 0), stop=(k == 7)),
                           reads=[R_hT[bb] for bb in range(lo // 128, hi // 128)] + [R_slot[s]], writes=[R_bank[bi]],
                           sig=(k == 7))

            def b1_post(c, bi):
                gi = c // 4
                w = WINDOWS[gi]
                si = c % 3
                ve = "pool" if (c % 4 == 3 or (c % 4 == 1 and gi in (1, 2))) else "dve"
                ei = 1 if ve == "pool" else 0
                p0, pA, pB = pbv[:, si, :], pbv[:, 3 + 2 * ei, :], pbv[:, 4 + 2 * ei, :]
                Rh, Rm = R_pb[si]
                RA, RB = R_pb[3 + ei]
                if ti == 0:
                    trk.op("act", lambda g: g.activation(out=p0[:, 0:16], in_=banks[bi][:, 0:16], func=AF.Copy, scale=0.0),
                           reads=[R_bank[bi]], writes=[Rh])
                else:
                    trk.op("act", lambda g: g.activation(out=p0[:, 0:16], in_=carry[:, lb, c, :], func=AF.Copy),
                           reads=[R_carry[lb][c]], writes=[Rh])
                trk.op("act", lambda g: g.activation(out=carry[:, lb, c, :], in_=banks[bi][:, T - 16:T], func=AF.Copy),
                       reads=[R_bank[bi]], writes=[R_carry[lb][c]])
                trk.op("act", lambda g: g.activation(out=p0[:, 16:PBW], in_=banks[bi][:], func=AF.Copy),
                       reads=[R_bank[bi]], writes=[Rm])
                free_bank(bi)
                steps = [(p0, pA, 1, [Rh, Rm], RA), (pA, pB, 2, [RA], RB), (pB, pA, 4, [RB], RA), (pA, pB, 8, [RA], RB)]
                nst = {2: 1, 4: 2, 8: 3, 16: 4}[w]
                lo = 0
                for (src, dst, sh, rr, wr) in steps[:nst]:
                    lo += sh
                    trk.op(ve, lambda g, src=src, dst=dst, sh=sh, lo=lo: g.tensor_tensor(
                        out=dst[:, lo:PBW], in0=src[:, lo:PBW], in1=src[:, lo - sh:PBW - sh], op=ALU.add),
                        reads=rr, writes=[wr])
                fin, Rf = steps[nst - 1][1], steps[nst - 1][4]
                trk.op("dve", lambda g: g.scalar_tensor_tensor(
                    out=pooledT[:, c, :], in0=fin[:, 16:PBW], scalar=1.0 / w, in1=p0[:, 16:PBW],
                    op0=ALU.mult, op1=ALU.subtract),
                    reads=[Rf, Rm], writes=[R_pooled[c]])
                if ti == 0:
                    tt = t16[:, ei, :]
                    trk.op(ve, lambda g: g.tensor_tensor(out=tt, in0=fin[:, 16:32], in1=invc[:, gi, :], op=ALU.mult),
                           reads=[Rf, R_const], writes=[R_t16[ei]])
                    trk.op(ve, lambda g: g.tensor_tensor(out=pooledT[:, c, 0:16], in0=tt, in1=p0[:, 16:32], op=ALU.subtract),
                           reads=[R_t16[ei], Rm, R_pooled[c]], writes=[R_pooled[c]])

            run_late()
            for jj in range(4):
                s = w_get()
                for cc in range(4):
                    bi = next_bank()
                    b1_mm(s, cc, bi, 0, 512)
                    b1_post(jj * 4 + cc, bi)
                w_done()
            if dbg:
                trk.sem("dbg")
                trk.op("pool", lambda g: g.dma_start(out=d_hT, in_=hT[:]), reads=R_hT, sem="dbg", inc=16)
                trk.op("pool", lambda g: g.dma_start(out=d_pl, in_=pooledT), reads=R_pooled, sem="dbg", inc=16)
            sg = None
            szs = None
            for c in range(16):
                gi = c // 4
                cc = c % 4
                ts = c % 2
                if c % 4 == 0:
                    sg = w_get()
                    szs = w_get()
                bm, bz = next_bank(), next_bank()
                for k in range(4):
                    trk.op("pe", lambda g, k=k, gi=gi, cc=cc: g.matmul(
                        banks[bm][:], lhsT=wring[:, sg, k, cc * 128:(cc + 1) * 128],
                        rhs=pooledT[:, gi * 4 + k, :], start=(k == 0), stop=(k == 3)),
                        reads=[R_pooled[gi * 4 + k], R_slot[sg]], writes=[R_bank[bm]], sig=(k == 3))
                for k in range(8):
                    trk.op("pe", lambda g, k=k, cc=cc: g.matmul(banks[bz][:], lhsT=wring[:, szs, k, cc * 128:(cc + 1) * 128],
                                                                rhs=hT[:, k, :], start=(k == 0), stop=(k == 7)),
                           reads=R_hT + [R_slot[szs]], writes=[R_bank[bz]], sig=(k == 7))
                zv = tmpA[:, ts, :]
                mvv = tmpA[:, 2 + ts, :]
                trk.op("act", lambda g, zv=zv: g.activation(out=zv, in_=banks[bz][:], func=AF.Silu),
                       reads=[R_bank[bz]], writes=[R_tmp[ts]])
                free_bank(bz)
                trk.op("act", lambda g, c=c, mvv=mvv: g.activation(out=mvv, in_=banks[bm][:], func=AF.Identity, scale=bscale[:, j, c:c + 1]),
                       reads=[R_bank[bm], R_const], writes=[R_tmp[2 + ts]])
                trk.op("pool", lambda g, c=c, zv=zv, mvv=mvv: g.tensor_tensor(out=yT[:, c, :], in0=mvv, in1=zv, op=ALU.mult),
                       reads=[R_tmp[2 + ts], R_tmp[ts]], writes=[R_yT[c]])
                free_bank(bm)
                if c % 4 == 3:
                    w_done(2)
            out_proj_and_norm(xu, l, R_yT, nxt)

        xt_d = x_d.rearrange("(n b p) d -> n p b d", b=NB, p=128)
        ot_d = out_d.rearrange("(n b p) d -> n p b d", b=NB, p=128)

        def load_x(ti):
            u = ti % 2
            trk.op("pool", lambda g: g.dma_start(out=xres[:, u], in_=xt_d[ti]), writes=R_x[u], sem=f"xl{u}", inc=16)

        if dbg:
            trk.sem("dbg")
            trk.op("pool", lambda g: g.dma_start(out=d_B, in_=Bhalf[:]), sem="dbg", inc=16)
            trk.op("pool", lambda g: g.dma_start(out=d_g, in_=ghalf[:]), sem="dbg", inc=16)
            trk.op("pool", lambda g: g.dma_start(out=d_ws, in_=wsT[:]), sem="dbg", inc=16)
        load_x(0)
        w_prefetch()
        if True:
            ws32 = big[:, 0:1024].rearrange("p (g q) -> p g q", g=8)
            bsbc = big[:, 1024:2048].rearrange("p (g q) -> p g q", g=8)
            ones32 = big[:, 2048:2176]
            gcol = big[:, 2176:2208].rearrange("p (j c) -> p j c", j=2)
            bcol = big[:, 2208:2240].rearrange("p (j c) -> p j c", j=2)
            rs_ps = [banks[0], banks[1]]
            R_ws32, R_ones, R_bsbc, R_gcol, R_bcol = (Region(n) for n in ("ws32", "ones", "bsbc", "gcol", "bcol"))
            R_rs = [Region("rs0"), Region("rs1")]
            R_c = {n: Region(n) for n in ("ident", "eps", "wsT", "Bhalf", "ghalf", "bscale", "invc")}
            ld = lambda out, in_, w: trk.op("sp", lambda g: g.dma_start(out=out, in_=in_), writes=[w],
                                            sem=trk.sem("sl_" + w.name), inc=16)

            trk.op("pool", lambda g: g.memset(ident[:], 0.0), writes=[R_c["ident"]])
            trk.op("pool", lambda g: g.affine_select(out=ident[:], in_=ident[:], compare_op=ALU.not_equal, fill=1.0,
                                                      base=0, pattern=[[-1, 128]], channel_multiplier=1),
                   reads=[R_c["ident"]], writes=[R_c["ident"]])
            trk.op("dve", lambda g: g.memset(eps_t[:], EPS), writes=[R_c["eps"]])
            trk.op("dve", lambda g: g.memset(ones32, 1.0), writes=[R_ones])
            for gi, w in enumerate(WINDOWS):
                trk.op("dve", lambda g, gi=gi, w=w: g.memset(invc[:, gi, :], 1.0 / w), writes=[R_c["invc"]])
                for t in range(w - 1):
                    trk.op("dve", lambda g, gi=gi, t=t: g.memset(invc[:, gi, t:t + 1], 1.0 / (t + 1)), writes=[R_c["invc"]])
            ld(gcol, a_ln_g.rearrange("j p c -> p j c"), R_gcol)
            ld(bcol, a_ln_b.rearrange("j p c -> p j c"), R_bcol)
            ld(bscale[:], b_scale.rearrange("j p c -> p j c"), R_c["bscale"])
            trk.op("dve", lambda g: g.tensor_scalar(out=ghalf[:], in0=gcol, scalar1=0.5, scalar2=None, op0=ALU.mult),
                   reads=[R_gcol], writes=[R_c["ghalf"]])
            for j in range(2):
                ld(ws32, a_w_s[j], R_ws32)
                ld(bsbc.rearrange("p g q -> p (g q)"), a_b_s[j].partition_broadcast(128), R_bsbc)
                trk.op("dve", lambda g: g.memset(ws32[64:128, :, 0:64], 0.0), reads=[R_ws32], writes=[R_ws32])
                trk.op("dve", lambda g, j=j: g.tensor_copy(out=wsT[:, j], in_=ws32), reads=[R_ws32], writes=[R_c["wsT"]])
                for hh in range(2):
                    trk.op("pe", lambda g, hh=hh: g.matmul(rs_ps[hh][:], lhsT=ones32,
                                                          rhs=ws32[:, hh * 4:(hh + 1) * 4, :].rearrange("p g q -> p (g q)"),
                                                          start=True, stop=True),
                           reads=[R_ones, R_ws32], writes=[R_rs[hh]])
                for cc in range(16):
                    gg = cc // 2
                    rsv = rs_ps[gg // 4][:, (gg % 4) * 128:(gg % 4 + 1) * 128]
                    trk.op("dve", lambda g, j=j, cc=cc, gg=gg, rsv=rsv: g.scalar_tensor_tensor(
                        out=Bhalf[:, j, cc, :], in0=rsv, scalar=bcol[:, j, cc:cc + 1], in1=bsbc[:, gg, :],
                        op0=ALU.mult, op1=ALU.add),
                        reads=[R_rs[gg // 4], R_bcol, R_bsbc], writes=[R_c["Bhalf"]])
                trk.op("dve", lambda g, j=j: g.tensor_scalar(out=Bhalf[:, j], in0=Bhalf[:, j], scalar1=0.5, scalar2=None,
                                                            op0=ALU.mult),
                       reads=[R_c["Bhalf"]], writes=[R_c["Bhalf"]])
            trk.barrier()

        for ti in range(n_tiles):
            xu = ti % 2
            if ti == 0:
                load_gain(layers[0], 0)
                for bl in ([0, 1], [2, 3]):
                    pre_A(xu, bl, layers[0])
                    pre_B(bl)
            if ti + 1 < n_tiles:
                load_x(ti + 1)
            for li, l in enumerate(layers):
                last = (li == len(layers) - 1)
                load_gain(l, 1)
                if not last:
                    load_gain(layers[li + 1], 0)
                    nxt = (xu, layers[li + 1])
                elif ti + 1 < n_tiles:
                    load_gain(layers[0], 0)
                    nxt = ((ti + 1) % 2, layers[0])
                else:
                    nxt = None
                if l % 2 == 0:
                    layer_A(xu, l, nxt)
                else:
                    layer_B(xu, l, ti, nxt)
            trk.op("pool", lambda g: g.dma_start(out=ot_d[ti], in_=xres[:, xu]), reads=R_x[xu], sem=f"xs{xu}", inc=16)
        for s in [f"xs{u}" for u in range(2)] + (["dbg"] if dbg else []):
            if trk.cnt.get(s, 0) > 0:
                nc.gpsimd.wait_ge(trk.sems[s], trk.cnt[s])
    return nc


def prep_inputs(inputs, b, n_tiles=8):
    f = lambda a: np.ascontiguousarray(np.asarray(a, dtype=np.float32))
    col = lambda v: f(np.asarray(v).reshape(v.shape[0], 16, 128).transpose(0, 2, 1))
    return {
        "x": f(np.asarray(inputs["x"])[b, :n_tiles * T]),
        "norm_pre": f(inputs["norm_pre"]),
        "norm_post": f(inputs["norm_post"]),
        "a_w_in": f(inputs["a_w_in"]),
        "a_ln_g": col(inputs["a_ln_g"]),
        "a_ln_b": col(inputs["a_ln_b"]),
        "a_w_s": f(np.asarray(inputs["a_w_s"]).transpose(0, 3, 1, 2)),
        "a_b_s": f(np.asarray(inputs["a_b_s"]).reshape(2, 1024)),
        "a_w_out": f(inputs["a_w_out"]),
        "b_w_in": f(inputs["b_w_in"]),
        "b_w_grp": f(inputs["b_w_grp"]),
        "b_scale": col(inputs["b_scale"]),
        "b_w_out": f(inputs["b_w_out"]),
    }


_NC_CACHE = {}


def kernel(**inputs):
    n = 8
    if "nc" not in _NC_CACHE:
        _NC_CACHE["nc"] = build_program()
    nc = _NC_CACHE["nc"]
    shared = prep_inputs(inputs, 0)
    in_maps = []
    xs = np.asarray(inputs["x"], dtype=np.float32)
    for b in range(n):
        m = dict(shared)
        m["x"] = np.ascontiguousarray(xs[b])
        in_maps.append(m)
    res = run_bass_kernel_spmd(nc, in_maps, core_ids=list(range(n)))
    out = np.stack([np.asarray(r["out"], dtype=np.float32) for r in res.results], axis=0)
    return out
```

```python
from contextlib import ExitStack

import numpy as np
import concourse.bass as bass
import concourse.mybir as mybir
from concourse.bass_utils import run_bass_kernel_spmd

F32 = mybir.dt.float32
BF16 = mybir.dt.bfloat16
AF = mybir.ActivationFunctionType
ALU = mybir.AluOpType

D = 1024
E = 2048
T = 512
NB = 4
DEPTH = 4
EPS = 1e-6
NSLOT = 5
WINDOWS = (2, 4, 8, 16)
NPA = 16
NPB = 16


class Region:
    __slots__ = ("name", "lw", "rd", "const")

    def __init__(self, name, const=False):
        self.name = name
        self.lw = None
        self.rd = []
        self.const = const


class Trk:
    def __init__(self, nc, es):
        self.nc = nc
        self.es = es
        self.eng = {"pe": nc.tensor, "act": nc.scalar, "dve": nc.vector, "pool": nc.gpsimd, "sp": nc.sync}
        self.sems = {}
        self.cnt = {}
        self.waited = {e: {} for e in self.eng}
        self.pend = {e: ([], []) for e in self.eng}
        self.esem = {}
        for e in ("pe", "act", "dve", "pool"):
            self.esem[e] = self.sem("c_" + e)

    def sem(self, name):
        if name not in self.sems:
            self.sems[name] = self.es.enter_context(self.nc.semaphore(name))
            self.cnt[name] = 0
        return name

    def op(self, e, fn, reads=(), writes=(), sig=True, sem=None, inc=1):
        deps = {}

        def add(ev):
            if ev is None:
                return
            s, v = ev
            if deps.get(s, 0) < v:
                deps[s] = v

        for r in reads:
            add(r.lw)
        for w in writes:
            add(w.lw)
            for ev in w.rd:
                add(ev)
        eng = self.eng[e]
        wd = self.waited[e]
        for s, v in deps.items():
            if wd.get(s, 0) < v:
                eng.wait_ge(self.sems[s], v)
                wd[s] = v
        inst = fn(eng)
        pr, pw = self.pend[e]
        pr.extend(r for r in reads if not r.const)
        pw.extend(writes)
        if sig:
            s = sem or self.esem[e]
            self.cnt[s] += inc
            inst.then_inc(self.sems[s], inc)
            ev = (s, self.cnt[s])
            for r in pr:
                r.rd.append(ev)
            for w in pw:
                w.lw = ev
                w.rd = []
            pr.clear()
            pw.clear()
        return inst

    def barrier(self, engines=("pe", "act", "dve", "pool", "sp")):
        for e in engines:
            for s, c in self.cnt.items():
                if c > 0 and self.waited[e].get(s, 0) < c:
                    self.eng[e].wait_ge(self.sems[s], c)
                    self.waited[e][s] = c


def handoff(group, new):
    evs = []
    for r in group:
        if r.lw is not None:
            evs.append(r.lw)
        evs.extend(r.rd)
    for r in new:
        r.rd.extend(evs)


def build_program(n_tiles=8, layers=(0, 1, 2, 3), dbg=False):
    ntok = n_tiles * T
    nc = bass.Bass("TRN2", target_bir_lowering=False)
    dt_in = lambda name, shape: nc.dram_tensor(name, shape, F32, kind="ExternalInput").ap()
    x_d = dt_in("x", [ntok, D])
    npre_d = dt_in("norm_pre", [DEPTH, D])
    npost_d = dt_in("norm_post", [DEPTH, D])
    a_w_in = dt_in("a_w_in", [2, D, 3 * E])
    a_ln_g = dt_in("a_ln_g", [2, 128, 16])
    a_ln_b = dt_in("a_ln_b", [2, 128, 16])
    a_w_s = dt_in("a_w_s", [2, 128, 8, 128])
    a_b_s = dt_in("a_b_s", [2, 1024])
    a_w_out = dt_in("a_w_out", [2, E, D])
    b_w_in = dt_in("b_w_in", [2, D, 2 * E])
    b_w_grp = dt_in("b_w_grp", [2, 4, 512, 512])
    b_scale = dt_in("b_scale", [2, 128, 16])
    b_w_out = dt_in("b_w_out", [2, E, D])
    out_d = nc.dram_tensor("out", [ntok, D], F32, kind="ExternalOutput").ap()
    scr = nc.dram_tensor("scr", [2 * NPA + 2 * NPB, 128, 4096], BF16, kind="Internal").ap()
    if dbg:
        d_hT = nc.dram_tensor("d_hT", [128, 8, T], BF16, kind="ExternalOutput").ap()
        d_vln = nc.dram_tensor("d_vln", [128, NB, E], BF16, kind="ExternalOutput").ap()
        d_yT = nc.dram_tensor("d_yT", [128, 16, T], BF16, kind="ExternalOutput").ap()
        d_ob = nc.dram_tensor("d_ob", [128, NB, D], F32, kind="ExternalOutput").ap()
        d_pl = nc.dram_tensor("d_pl", [128, 16, T], BF16, kind="ExternalOutput").ap()
        d_B = nc.dram_tensor("d_B", [128, 2, 16, 128], F32, kind="ExternalOutput").ap()
        d_tmp = nc.dram_tensor("d_tmp", [128, 2, 3, T], F32, kind="ExternalOutput").ap()
        d_g = nc.dram_tensor("d_g", [128, 2, 16], F32, kind="ExternalOutput").ap()
        d_ws = nc.dram_tensor("d_ws", [128, 2, 8, 128], BF16, kind="ExternalOutput").ap()

    def a_pieces(j):
        win = a_w_in[j].rearrange("(k p) c -> p k c", p=128)
        wout = a_w_out[j].rearrange("(k p) c -> p k c", p=128)
        P = []
        for cg in range(4):
            P.append((win[:, :, E + cg * 512:E + (cg + 1) * 512], 8))
        for jj in range(4):
            P.append((win[:, :, jj * 512:(jj + 1) * 512], 8))
            P.append((win[:, :, 2 * E + jj * 512:2 * E + (jj + 1) * 512], 8))
        for kh in range(2):
            for ch in range(2):
                P.append((wout[:, kh * 8:(kh + 1) * 8, ch * 512:(ch + 1) * 512], 8))
        return P

    def b_pieces(j):
        win = b_w_in[j].rearrange("(k p) c -> p k c", p=128)
        wg = b_w_grp[j].rearrange("g (k p) c -> p g k c", p=128)
        wout = b_w_out[j].rearrange("(k p) c -> p k c", p=128)
        P = [(win[:, :, jj * 512:(jj + 1) * 512], 8) for jj in range(4)]
        for gi in range(4):
            P.append((wg[:, gi], 4))
            P.append((win[:, :, E + gi * 512:E + (gi + 1) * 512], 8))
        for kh in range(2):
            for ch in range(2):
                P.append((wout[:, kh * 8:(kh + 1) * 8, ch * 512:(ch + 1) * 512], 8))
        return P

    layer_base = {}
    layer_pieces = {}
    base = 0
    for l in range(DEPTH):
        layer_base[l] = base
        layer_pieces[l] = a_pieces(l // 2) if l % 2 == 0 else b_pieces(l // 2)
        base += len(layer_pieces[l])

    es = ExitStack()
    with es:
        trk = Trk(nc, es)
        sb = lambda name, shape, dt: es.enter_context(nc.sbuf_tensor(name, shape, dt))

        ident = sb("ident", [128, 128], BF16)
        eps_t = sb("eps_t", [128, 1], F32)
        wsT = sb("wsT", [128, 2, 8, 128], BF16)
        Bhalf = sb("Bhalf", [128, 2, 16, 128], F32)
        ghalf = sb("ghalf", [128, 2, 16], F32)
        bscale = sb("bscale", [128, 2, 16], F32)
        invc = sb("invc", [128, 4, 16], F32)
        carry = sb("carry", [128, 2, 16, 16], F32)
        R_const = Region("const", const=True)
        R_carry = [[Region(f"carry{lb}_{c}") for c in range(16)] for lb in range(2)]

        R_scr = {}
        xres = sb("xres", [128, 2, NB, D], F32)
        hT = sb("hT", [128, 8, T], BF16)
        h_tm = sb("h_tm", [128, 2, D], BF16)
        big = sb("big", [128, 8192], F32)
        vln = sb("vln", [128, NB, E], BF16)
        yT = sb("yT", [128, 16, T], BF16)
        wring = sb("wring", [128, NSLOT, 8, 512], BF16)
        tmpA = sb("tmpA", [128, 8, T], F32)
        gains = sb("gains", [128, 2, 2, D], F32)
        junk = sb("junk", [128, D], BF16)
        stats = sb("stats", [128, NB, 4, 6], F32)
        mv = sb("mv", [128, NB, 2], F32)
        nrm = sb("nrm", [128, 8, NB], F32)
        t16 = sb("t16", [128, 2, 16], F32)
        sso = sb("sso", [128, 2, NB], F32)
        banks = [es.enter_context(nc.psum_tensor(f"bank{i}", [128, 512], F32)) for i in range(8)]

        vbuf = big[:].rearrange("p (b e) -> p b e", b=NB)
        obuf = big[:, 0:4096].rearrange("p (b d) -> p b d", b=NB)
        pooledT = big[:, 0:4096].bitcast(BF16).rearrange("p (c t) -> p c t", c=16)
        PBW = 528
        pbv = big[:, 4096:4096 + 7 * PBW].rearrange("p (i w) -> p i w", i=7)

        R_x = [[Region(f"x{u}_{b}") for b in range(NB)] for u in range(2)]
        R_hT = [Region(f"hT{b}") for b in range(NB)]
        R_htm = [Region(f"htm{i}") for i in range(2)]
        R_vbuf = [[Region(f"vbuf{b}_{cg}") for cg in range(4)] for b in range(NB)]
        R_obuf = [[Region(f"obuf{b}_{ch}") for ch in range(2)] for b in range(NB)]
        R_pooled = [Region(f"pooled{c}") for c in range(16)]
        R_pb = [[Region(f"pb{s}_{i}") for i in range(2)] for s in range(3)] + \
               [[Region(f"pbt{e}_{i}") for i in range(2)] for e in range(2)]
        big_group = [r for row in R_vbuf for r in row] + [r for row in R_obuf for r in row] + R_pooled + \
                    [r for row in R_pb for r in row]
        R_vln = [Region(f"vln{b}") for b in range(NB)]
        R_yT = [Region(f"yT{c}") for c in range(16)]
        R_slot = [Region(f"slot{s}") for s in range(NSLOT)]
        R_tmp = [Region(f"tmp{i}") for i in range(8)]
        R_gain = [[Region(f"gain{u}_{i}") for i in range(2)] for u in range(2)]
        R_junk = Region("junk")
        R_stats = [[Region(f"st{b}_{cg}") for cg in range(4)] for b in range(NB)]
        R_mv = [Region(f"mv{b}") for b in range(NB)]
        R_nrm = [[Region(f"nrm{i}_{b}") for b in range(NB)] for i in range(8)]
        R_t16 = [Region("t16_0"), Region("t16_1")]
        R_sso = [[Region(f"sso{ch}_{b}") for b in range(NB)] for ch in range(2)]
        R_bank = [Region(f"bank{i}") for i in range(8)]
        for s in range(NSLOT):
            trk.sem(f"w{s}")
        for u in range(2):
            trk.sem(f"xl{u}")
            trk.sem(f"xs{u}")
            trk.sem(f"gl{u}0")
            trk.sem(f"gl{u}1")

        bank_free = list(range(8))

        def next_bank():
            assert bank_free, "out of PSUM banks"
            return bank_free.pop(0)

        def free_bank(i):
            assert i not in bank_free
            bank_free.append(i)

        wseq = []
        for ti in range(n_tiles):
            g_ = 0
            for l in layers:
                for i in range(len(layer_pieces[l])):
                    wseq.append((ti, layer_base[l] + i, layer_pieces[l][i][1], layer_pieces[l][i][0], g_))
                    g_ += 1
        n_pt = len(wseq) // n_tiles
        wst = {"load": 0, "use": 0, "done": 0, "conv": 0}
        R_scrp = {}
        KC = 16
        LOOK = 12
        for k_ in range(KC):
            trk.sem(f"cv{k_}")

        def conv_upto(n):
            while wst["conv"] < min(n, n_pt):
                g_ = wst["conv"]
                _, si, nk, src, _ = wseq[g_]
                sname = f"cv{g_ % KC}"
                dst = scr[si][:, 0:nk * 512].rearrange("p (k c) -> p k c", c=512)
                nc.gpsimd.dma_start(out=dst, in_=src).then_inc(trk.sems[sname], 16)
                trk.cnt[sname] += 16
                R_scrp[si] = Region(f"scrp{si}")
                R_scrp[si].lw = (sname, trk.cnt[sname])
                wst["conv"] += 1

        def w_prefetch():
            while wst["load"] < len(wseq) and wst["load"] < wst["done"] + NSLOT:
                i = wst["load"]
                s = i % NSLOT
                ti_, si, nk, src, g_ = wseq[i]
                if ti_ == 0:
                    conv_upto(g_ + 1)
                trk.op("sp", lambda g, s=s, si=si, nk=nk: g.dma_start(
                    out=wring[:, s, 0:nk].rearrange("p k c -> p (k c)"), in_=scr[si][:, 0:nk * 512]),
                    reads=[R_scrp[si]], writes=[R_slot[s]], sem=f"w{s}", inc=16)
                wst["load"] += 1

        def w_get():
            w_prefetch()
            assert wst["load"] > wst["use"], "weight ring deadlock"
            s = wst["use"] % NSLOT
            wst["use"] += 1
            return s

        def w_done(n=1):
            wst["done"] += n
            if wst["done"] <= n_pt:
                conv_upto(wst["done"] + LOOK)
            w_prefetch()

        def rsqrt_into(dst_ap, dst_reg, src_ap, src_reg, scale):
            trk.op("act", lambda g: g.activation(out=dst_ap, in_=src_ap, func=AF.Sqrt, bias=eps_t[:, 0:1], scale=scale),
                   reads=[src_reg, R_const], writes=[dst_reg])
            trk.op("dve", lambda g: g.reciprocal(out=dst_ap, in_=dst_ap), reads=[dst_reg], writes=[dst_reg])

        def load_gain(l, which):
            u = l % 2
            src = npre_d if which == 0 else npost_d
            trk.op("sp", lambda g: g.dma_start(out=gains[:, u, which, :], in_=src[l].partition_broadcast(128)),
                   writes=[R_gain[u][which]], sem=f"gl{u}{which}", inc=16)

        def rsqrt_cols(dst_row, src_row, b0, b1, scale):
            regs_s = [R_nrm[src_row][b] for b in range(b0, b1)]
            regs_d = [R_nrm[dst_row][b] for b in range(b0, b1)]
            trk.op("act", lambda g: g.activation(out=nrm[:, dst_row, b0:b1], in_=nrm[:, src_row, b0:b1], func=AF.Sqrt,
                                                 bias=eps_t[:, 0:1], scale=scale),
                   reads=regs_s + [R_const], writes=regs_d)
            trk.op("dve", lambda g: g.reciprocal(out=nrm[:, dst_row, b0:b1], in_=nrm[:, dst_row, b0:b1]),
                   reads=regs_d, writes=regs_d)

        pending_late = []

        def run_late():
            while pending_late:
                pending_late.pop(0)()

        def pre_A(xu, bl, l):
            u = l % 2
            for b in bl:
                xv = xres[:, xu, b, :]
                trk.op("act", lambda g, b=b, xv=xv: g.activation(out=junk[:], in_=xv, func=AF.Square, accum_out=nrm[:, 2, b:b + 1]),
                       reads=[R_x[xu][b]], writes=[R_nrm[2][b]])
            for b in bl:
                rsqrt_cols(3, 2, b, b + 1, 1.0 / D)
            for b in bl:
                hb = b % 2
                xv = xres[:, xu, b, :]
                trk.op("dve", lambda g, b=b, xv=xv, hb=hb: g.scalar_tensor_tensor(out=h_tm[:, hb, :], in0=xv, scalar=nrm[:, 3, b:b + 1],
                                                                                  in1=gains[:, u, 0, :], op0=ALU.mult, op1=ALU.mult),
                       reads=[R_x[xu][b], R_nrm[3][b], R_gain[u][0]], writes=[R_htm[hb]])

        def pre_B(bl):
            for b in bl:
                hb = b % 2
                bi = next_bank()
                tp = banks[bi][:].bitcast(BF16).rearrange("p (k t) -> p k t", t=128)
                for k in range(8):
                    trk.op("pe", lambda g, k=k, tp=tp, hb=hb: g.transpose(tp[:, k, :], h_tm[:, hb, k * 128:(k + 1) * 128], ident[:]),
                           reads=[R_htm[hb], R_const], writes=[R_bank[bi]], sig=(k == 7))
                if b % 2 == 0:
                    trk.op("act", lambda g, b=b, tp=tp: g.activation(out=hT[:, :, b * 128:(b + 1) * 128], in_=tp, func=AF.Copy),
                           reads=[R_bank[bi]], writes=[R_hT[b]])
                else:
                    trk.op("dve", lambda g, b=b, tp=tp: g.tensor_copy(out=hT[:, :, b * 128:(b + 1) * 128], in_=tp),
                           reads=[R_bank[bi]], writes=[R_hT[b]])
                free_bank(bi)

        def out_proj_and_norm(xu, l, yreg_all, nxt):
            u = l % 2
            handoff(big_group, [r for row in R_obuf for r in row])
            if dbg:
                trk.sem("dbg")
                trk.op("pool", lambda g: g.dma_start(out=d_yT, in_=yT[:]), reads=R_yT, sem="dbg", inc=16)
            ss = [w_get() for _ in range(4)]
            groups = [[0, 1], [2, 3]]
            bis = [[None] * NB for _ in range(2)]

            def mm(half, release):
                for ch in range(2):
                    for b in half:
                        bis[ch][b] = next_bank()
                for kh in range(2):
                    for ch in range(2):
                        s = ss[kh * 2 + ch]
                        for b in half:
                            for k in range(8):
                                first = (kh == 0 and k == 0)
                                last = (kh == 1 and k == 7)
                                trk.op("pe", lambda g, b=b, k=k, s=s, kh=kh, ch=ch, first=first, last=last: g.matmul(
                                    banks[bis[ch][b]][:], lhsT=yT[:, kh * 8 + k, b * 128:(b + 1) * 128], rhs=wring[:, s, k, :],
                                    start=first, stop=last),
                                    reads=([yreg_all[kh * 8 + kk] for kk in range(8)] + [R_slot[s]]) if k == 0 else [R_slot[s]], writes=[R_bank[bis[ch][b]]],
                                    sig=(last or (b == half[-1] and k == 7)))
                        if release:
                            w_done()

            def post(bl):
                b0, b1 = bl[0], bl[-1] + 1
                for ch in range(2):
                    for b in bl:
                        trk.op("act", lambda g, b=b, ch=ch: g.activation(out=junk[:, 0:512], in_=banks[bis[ch][b]][:], func=AF.Square,
                                                                          accum_out=sso[:, ch, b:b + 1]),
                               reads=[R_bank[bis[ch][b]]], writes=[R_sso[ch][b]])
                trk.op("dve", lambda g: g.tensor_tensor(out=nrm[:, 0, b0:b1], in0=sso[:, 0, b0:b1], in1=sso[:, 1, b0:b1], op=ALU.add),
                       reads=[R_sso[c][b] for c in range(2) for b in bl], writes=[R_nrm[0][b] for b in bl])
                rsqrt_cols(1, 0, b0, b1, 1.0 / D)
                for b in bl:
                    xv = xres[:, xu, b, :]
                    for ch in range(2):
                        trk.op("dve", lambda g, b=b, ch=ch: g.scalar_tensor_tensor(
                            out=obuf[:, b, ch * 512:(ch + 1) * 512], in0=banks[bis[ch][b]][:], scalar=nrm[:, 1, b:b + 1],
                            in1=gains[:, u, 1, ch * 512:(ch + 1) * 512], op0=ALU.mult, op1=ALU.mult),
                            reads=[R_bank[bis[ch][b]], R_nrm[1][b], R_gain[u][1]], writes=[R_obuf[b][ch]])
                    free_bank(bis[0][b])
                    free_bank(bis[1][b])
                    trk.op("dve", lambda g, b=b, xv=xv: g.tensor_tensor(out=xv, in0=obuf[:, b, :], in1=xv, op=ALU.add),
                           reads=R_obuf[b] + [R_x[xu][b]], writes=[R_x[xu][b]])

            for gi, g in enumerate(groups):
                mm(g, gi == len(groups) - 1)
                if gi > 0 and nxt is not None:
                    pre_B(groups[gi - 1])
                post(g)
                if nxt is not None:
                    pre_A(nxt[0], g, nxt[1])
            if nxt is not None:
                pending_late.append(lambda: pre_B(groups[-1]))

        def ln_all():
            for b in range(NB):
                trk.op("dve", lambda g, b=b: g.bn_aggr(out=mv[:, b, :], in_=stats[:, b].rearrange("p c s -> p (c s)")),
                       reads=R_stats[b], writes=[R_mv[b]])
            trk.op("act", lambda g: g.activation(out=nrm[:, 4, :], in_=mv[:, :, 1], func=AF.Sqrt, bias=eps_t[:, 0:1], scale=1.0),
                   reads=R_mv + [R_const], writes=R_nrm[4])
            trk.op("dve", lambda g: g.reciprocal(out=nrm[:, 4, :], in_=nrm[:, 4, :]), reads=R_nrm[4], writes=R_nrm[4])
            trk.op("dve", lambda g: g.scalar_tensor_tensor(out=nrm[:, 5, :], in0=mv[:, :, 0], scalar=-1.0, in1=nrm[:, 4, :],
                                                           op0=ALU.mult, op1=ALU.mult),
                   reads=R_mv + R_nrm[4], writes=R_nrm[5])
            for b in range(NB):
                if b % 2 == 0:
                    trk.op("dve", lambda g, b=b: g.tensor_scalar(out=vln[:, b, :], in0=vbuf[:, b, :], scalar1=mv[:, b, 0:1],
                                                                 scalar2=nrm[:, 4, b:b + 1], op0=ALU.subtract, op1=ALU.mult),
                           reads=R_vbuf[b] + [R_mv[b], R_nrm[4][b]], writes=[R_vln[b]])
                else:
                    trk.op("act", lambda g, b=b: g.activation(out=vln[:, b, :], in_=vbuf[:, b, :], func=AF.Identity,
                                                              bias=nrm[:, 5, b:b + 1], scale=nrm[:, 4, b:b + 1]),
                           reads=R_vbuf[b] + [R_nrm[5][b], R_nrm[4][b]], writes=[R_vln[b]])

        def layer_A(xu, l, nxt):
            j = l // 2
            handoff(big_group, [r for row in R_vbuf for r in row])
            sv4 = [w_get() for _ in range(4)]

            def a1_job(cg, b):
                s = sv4[cg]
                bi = next_bank()
                for k in range(8):
                    trk.op("pe", lambda g, k=k: g.matmul(banks[bi][:], lhsT=hT[:, k, b * 128:(b + 1) * 128], rhs=wring[:, s, k, :],
                                                         start=(k == 0), stop=(k == 7)),
                           reads=[R_hT[b], R_slot[s]], writes=[R_bank[bi]], sig=(k == 7))
                vv = vbuf[:, b, cg * 512:(cg + 1) * 512]
                trk.op("act", lambda g: g.activation(out=vv, in_=banks[bi][:], func=AF.Gelu_apprx_tanh),
                       reads=[R_bank[bi]], writes=[R_vbuf[b][cg]])
                free_bank(bi)
                trk.op("dve", lambda g: g.bn_stats(out=stats[:, b, cg, :], in_=vv),
                       reads=[R_vbuf[b][cg]], writes=[R_stats[b][cg]])

            for cg in range(4):
                for b in (0, 1):
                    a1_job(cg, b)
            run_late()
            for cg in range(4):
                for b in (2, 3):
                    a1_job(cg, b)
                w_done()
            ln_all()
            if dbg:
                trk.sem("dbg")
                trk.op("pool", lambda g: g.dma_start(out=d_hT, in_=hT[:]), reads=R_hT, sem="dbg", inc=16)
                trk.op("pool", lambda g: g.dma_start(out=d_vln, in_=vln[:]), reads=R_vln, sem="dbg", inc=16)
            LAG = 3
            pend = []

            def finish(c):
                gg = c // 2
                ub = c % 4
                tb = c % 2
                bm = next_bank()
                for b in range(NB):
                    trk.op("pe", lambda g, b=b: g.matmul(banks[bm][:, b * 128:(b + 1) * 128],
                                                         lhsT=vln[:, b, c * 128:(c + 1) * 128],
                                                         rhs=wsT[:, j, gg, :], start=True, stop=True),
                           reads=[R_vln[b], R_const], writes=[R_bank[bm]], sig=(b == NB - 1))
                t1 = tmpA[:, 6 + tb, :]
                uv = tmpA[:, ub, :]
                trk.op("dve", lambda g: g.scalar_tensor_tensor(
                    out=t1.rearrange("p (b t) -> p b t", t=128),
                    in0=banks[bm][:].rearrange("p (b t) -> p b t", t=128),
                    scalar=ghalf[:, j, c:c + 1],
                    in1=Bhalf[:, j, c:c + 1, :].to_broadcast([128, NB, 128]),
                    op0=ALU.mult, op1=ALU.add),
                    reads=[R_bank[bm], R_const], writes=[R_tmp[6 + tb]])
                free_bank(bm)
                trk.op("dve", lambda g: g.tensor_tensor(out=yT[:, c, :], in0=t1, in1=uv, op=ALU.mult),
                       reads=[R_tmp[6 + tb], R_tmp[ub]], writes=[R_yT[c]])

            for jj in range(4):
                su = w_get()
                sz = w_get()
                for cc in range(4):
                    c = jj * 4 + cc
                    ub = c % 4
                    tb = c % 2
                    bu, bz = next_bank(), next_bank()
                    for k in range(8):
                        trk.op("pe", lambda g, k=k, cc=cc: g.matmul(banks[bu][:], lhsT=wring[:, su, k, cc * 128:(cc + 1) * 128],
                                                                    rhs=hT[:, k, :], start=(k == 0), stop=(k == 7)),
                               reads=R_hT + [R_slot[su]], writes=[R_bank[bu]], sig=(k == 7))
                    for k in range(8):
                        trk.op("pe", lambda g, k=k, cc=cc: g.matmul(banks[bz][:], lhsT=wring[:, sz, k, cc * 128:(cc + 1) * 128],
                                                                    rhs=hT[:, k, :], start=(k == 0), stop=(k == 7)),
                               reads=R_hT + [R_slot[sz]], writes=[R_bank[bz]], sig=(k == 7))
                    uv, sv = tmpA[:, ub, :], tmpA[:, 4 + tb, :]
                    trk.op("act", lambda g, uv=uv: g.activation(out=uv, in_=banks[bu][:], func=AF.Gelu_apprx_tanh),
                           reads=[R_bank[bu]], writes=[R_tmp[ub]])
                    free_bank(bu)
                    trk.op("act", lambda g, sv=sv: g.activation(out=sv, in_=banks[bz][:], func=AF.Tanh, scale=0.5),
                           reads=[R_bank[bz]], writes=[R_tmp[4 + tb]])
                    trk.op("dve", lambda g, sv=sv: g.scalar_tensor_tensor(out=sv, in0=sv, scalar=1.0, in1=banks[bz][:],
                                                                         op0=ALU.add, op1=ALU.mult),
                           reads=[R_tmp[4 + tb], R_bank[bz]], writes=[R_tmp[4 + tb]])
                    free_bank(bz)
                    trk.op("pool", lambda g, uv=uv, sv=sv: g.tensor_tensor(out=uv, in0=uv, in1=sv, op=ALU.mult),
                           reads=[R_tmp[ub], R_tmp[4 + tb]], writes=[R_tmp[ub]])
                    pend.append(c)
                    if len(pend) > LAG:
                        finish(pend.pop(0))
                w_done(2)
            while pend:
                finish(pend.pop(0))
            out_proj_and_norm(xu, l, R_yT, nxt)

        def layer_B(xu, l, ti, nxt):
            j = l // 2
            lb = j
            handoff(big_group, R_pooled + [r for row in R_pb for r in row])
            def b1_mm(s, cc, bi, lo, hi):
                for k in range(8):
                    trk.op("pe", lambda g, k=k: g.matmul(banks[bi][:, lo:hi], lhsT=wring[:, s, k, cc * 128:(cc + 1) * 128],
                                                         rhs=hT[:, k, lo:hi], start=(k == 0), stop=(k == 7)),
                           reads=[R_hT[bb] for bb in range(lo // 128, hi // 128)] + [R_slot[s]], writes=[R_bank[bi]],
                           sig=(k == 7))

            def b1_post(c, bi):
                gi = c // 4
                w = WINDOWS[gi]
                si = c % 3
                ve = "pool" if (c % 4 == 3 or (c % 4 == 1 and gi in (1, 2))) else "dve"
                ei = 1 if ve == "pool" else 0
                p0, pA, pB = pbv[:, si, :], pbv[:, 3 + 2 * ei, :], pbv[:, 4 + 2 * ei, :]
                Rh, Rm = R_pb[si]
                RA, RB = R_pb[3 + ei]
                if ti == 0:
                    trk.op("act", lambda g: g.activation(out=p0[:, 0:16], in_=banks[bi][:, 0:16], func=AF.Copy, scale=0.0),
                           reads=[R_bank[bi]], writes=[Rh])
                else:
                    trk.op("act", lambda g: g.activation(out=p0[:, 0:16], in_=carry[:, lb, c, :], func=AF.Copy),
                           reads=[R_carry[lb][c]], writes=[Rh])
                trk.op("act", lambda g: g.activation(out=carry[:, lb, c, :], in_=banks[bi][:, T - 16:T], func=AF.Copy),
                       reads=[R_bank[bi]], writes=[R_carry[lb][c]])
                trk.op("act", lambda g: g.activation(out=p0[:, 16:PBW], in_=banks[bi][:], func=AF.Copy),
                       reads=[R_bank[bi]], writes=[Rm])
                free_bank(bi)
                steps = [(p0, pA, 1, [Rh, Rm], RA), (pA, pB, 2, [RA], RB), (pB, pA, 4, [RB], RA), (pA, pB, 8, [RA], RB)]
                nst = {2: 1, 4: 2, 8: 3, 16: 4}[w]
                lo = 0
                for (src, dst, sh, rr, wr) in steps[:nst]:
                    lo += sh
                    trk.op(ve, lambda g, src=src, dst=dst, sh=sh, lo=lo: g.tensor_tensor(
                        out=dst[:, lo:PBW], in0=src[:, lo:PBW], in1=src[:, lo - sh:PBW - sh], op=ALU.add),
                        reads=rr, writes=[wr])
                fin, Rf = steps[nst - 1][1], steps[nst - 1][4]
                trk.op("dve", lambda g: g.scalar_tensor_tensor(
                    out=pooledT[:, c, :], in0=fin[:, 16:PBW], scalar=1.0 / w, in1=p0[:, 16:PBW],
                    op0=ALU.mult, op1=ALU.subtract),
                    reads=[Rf, Rm], writes=[R_pooled[c]])
                if ti == 0:
                    tt = t16[:, ei, :]
                    trk.op(ve, lambda g: g.tensor_tensor(out=tt, in0=fin[:, 16:32], in1=invc[:, gi, :], op=ALU.mult),
                           reads=[Rf, R_const], writes=[R_t16[ei]])
                    trk.op(ve, lambda g: g.tensor_tensor(out=pooledT[:, c, 0:16], in0=tt, in1=p0[:, 16:32], op=ALU.subtract),
                           reads=[R_t16[ei], Rm, R_pooled[c]], writes=[R_pooled[c]])

            run_late()
            for jj in range(4):
                s = w_get()
                for cc in range(4):
                    bi = next_bank()
                    b1_mm(s, cc, bi, 0, 512)
                    b1_post(jj * 4 + cc, bi)
                w_done()
            if dbg:
                trk.sem("dbg")
                trk.op("pool", lambda g: g.dma_start(out=d_hT, in_=hT[:]), reads=R_hT, sem="dbg", inc=16)
                trk.op("pool", lambda g: g.dma_start(out=d_pl, in_=pooledT), reads=R_pooled, sem="dbg", inc=16)
            sg = None
            szs = None
            for c in range(16):
                gi = c // 4
                cc = c % 4
                ts = c % 2
                if c % 4 == 0:
                    sg = w_get()
                    szs = w_get()
                bm, bz = next_bank(), next_bank()
                for k in range(4):
                    trk.op("pe", lambda g, k=k, gi=gi, cc=cc: g.matmul(
                        banks[bm][:], lhsT=wring[:, sg, k, cc * 128:(cc + 1) * 128],
                        rhs=pooledT[:, gi * 4 + k, :], start=(k == 0), stop=(k == 3)),
                        reads=[R_pooled[gi * 4 + k], R_slot[sg]], writes=[R_bank[bm]], sig=(k == 3))
                for k in range(8):
                    trk.op("pe", lambda g, k=k, cc=cc: g.matmul(banks[bz][:], lhsT=wring[:, szs, k, cc * 128:(cc + 1) * 128],
                                                                rhs=hT[:, k, :], start=(k == 0), stop=(k == 7)),
                           reads=R_hT + [R_slot[szs]], writes=[R_bank[bz]], sig=(k == 7))
                zv = tmpA[:, ts, :]
                mvv = tmpA[:, 2 + ts, :]
                trk.op("act", lambda g, zv=zv: g.activation(out=zv, in_=banks[bz][:], func=AF.Silu),
                       reads=[R_bank[bz]], writes=[R_tmp[ts]])
                free_bank(bz)
                trk.op("act", lambda g, c=c, mvv=mvv: g.activation(out=mvv, in_=banks[bm][:], func=AF.Identity, scale=bscale[:, j, c:c + 1]),
                       reads=[R_bank[bm], R_const], writes=[R_tmp[2 + ts]])
                trk.op("pool", lambda g, c=c, zv=zv, mvv=mvv: g.tensor_tensor(out=yT[:, c, :], in0=mvv, in1=zv, op=ALU.mult),
                       reads=[R_tmp[2 + ts], R_tmp[ts]], writes=[R_yT[c]])
                free_bank(bm)
                if c % 4 == 3:
                    w_done(2)
            out_proj_and_norm(xu, l, R_yT, nxt)

        xt_d = x_d.rearrange("(n b p) d -> n p b d", b=NB, p=128)
        ot_d = out_d.rearrange("(n b p) d -> n p b d", b=NB, p=128)

        def load_x(ti):
            u = ti % 2
            trk.op("pool", lambda g: g.dma_start(out=xres[:, u], in_=xt_d[ti]), writes=R_x[u], sem=f"xl{u}", inc=16)

        if dbg:
            trk.sem("dbg")
            trk.op("pool", lambda g: g.dma_start(out=d_B, in_=Bhalf[:]), sem="dbg", inc=16)
            trk.op("pool", lambda g: g.dma_start(out=d_g, in_=ghalf[:]), sem="dbg", inc=16)
            trk.op("pool", lambda g: g.dma_start(out=d_ws, in_=wsT[:]), sem="dbg", inc=16)
        load_x(0)
        conv_upto(4)
        for k_ in range(4):
            nc.gpsimd.wait_ge(trk.sems[f"cv{k_}"], trk.cnt[f"cv{k_}"])
        conv_upto(LOOK)
        w_prefetch()
        if True:
            ws32 = big[:, 0:1024].rearrange("p (g q) -> p g q", g=8)
            bsbc = big[:, 1024:2048].rearrange("p (g q) -> p g q", g=8)
            ones32 = big[:, 2048:2176]
            gcol = big[:, 2176:2208].rearrange("p (j c) -> p j c", j=2)
            bcol = big[:, 2208:2240].rearrange("p (j c) -> p j c", j=2)
            rs_ps = [banks[0], banks[1]]
            R_ws32, R_ones, R_bsbc, R_gcol, R_bcol = (Region(n) for n in ("ws32", "ones", "bsbc", "gcol", "bcol"))
            R_rs = [Region("rs0"), Region("rs1")]
            R_c = {n: Region(n) for n in ("ident", "eps", "wsT", "Bhalf", "ghalf", "bscale", "invc")}
            ld = lambda out, in_, w: trk.op("sp", lambda g: g.dma_start(out=out, in_=in_), writes=[w],
                                            sem=trk.sem("sl_" + w.name), inc=16)

            trk.op("pool", lambda g: g.memset(ident[:], 0.0), writes=[R_c["ident"]])
            trk.op("pool", lambda g: g.affine_select(out=ident[:], in_=ident[:], compare_op=ALU.not_equal, fill=1.0,
                                                      base=0, pattern=[[-1, 128]], channel_multiplier=1),
                   reads=[R_c["ident"]], writes=[R_c["ident"]])
            trk.op("dve", lambda g: g.memset(eps_t[:], EPS), writes=[R_c["eps"]])
            trk.op("dve", lambda g: g.memset(ones32, 1.0), writes=[R_ones])
            for gi, w in enumerate(WINDOWS):
                trk.op("dve", lambda g, gi=gi, w=w: g.memset(invc[:, gi, :], 1.0 / w), writes=[R_c["invc"]])
                for t in range(w - 1):
                    trk.op("dve", lambda g, gi=gi, t=t: g.memset(invc[:, gi, t:t + 1], 1.0 / (t + 1)), writes=[R_c["invc"]])
            ld(gcol, a_ln_g.rearrange("j p c -> p j c"), R_gcol)
            ld(bcol, a_ln_b.rearrange("j p c -> p j c"), R_bcol)
            ld(bscale[:], b_scale.rearrange("j p c -> p j c"), R_c["bscale"])
            trk.op("dve", lambda g: g.tensor_scalar(out=ghalf[:], in0=gcol, scalar1=0.5, scalar2=None, op0=ALU.mult),
                   reads=[R_gcol], writes=[R_c["ghalf"]])
            for j in range(2):
                ld(ws32, a_w_s[j], R_ws32)
                ld(bsbc.rearrange("p g q -> p (g q)"), a_b_s[j].partition_broadcast(128), R_bsbc)
                trk.op("dve", lambda g: g.memset(ws32[64:128, :, 0:64], 0.0), reads=[R_ws32], writes=[R_ws32])
                trk.op("dve", lambda g, j=j: g.tensor_copy(out=wsT[:, j], in_=ws32), reads=[R_ws32], writes=[R_c["wsT"]])
                for hh in range(2):
                    trk.op("pe", lambda g, hh=hh: g.matmul(rs_ps[hh][:], lhsT=ones32,
                                                          rhs=ws32[:, hh * 4:(hh + 1) * 4, :].rearrange("p g q -> p (g q)"),
                                                          start=True, stop=True),
                           reads=[R_ones, R_ws32], writes=[R_rs[hh]])
                for cc in range(16):
                    gg = cc // 2
                    rsv = rs_ps[gg // 4][:, (gg % 4) * 128:(gg % 4 + 1) * 128]
                    trk.op("dve", lambda g, j=j, cc=cc, gg=gg, rsv=rsv: g.scalar_tensor_tensor(
                        out=Bhalf[:, j, cc, :], in0=rsv, scalar=bcol[:, j, cc:cc + 1], in1=bsbc[:, gg, :],
                        op0=ALU.mult, op1=ALU.add),
                        reads=[R_rs[gg // 4], R_bcol, R_bsbc], writes=[R_c["Bhalf"]])
                trk.op("dve", lambda g, j=j: g.tensor_scalar(out=Bhalf[:, j], in0=Bhalf[:, j], scalar1=0.5, scalar2=None,
                                                            op0=ALU.mult),
                       reads=[R_c["Bhalf"]], writes=[R_c["Bhalf"]])
            trk.barrier()

        for ti in range(n_tiles):
            xu = ti % 2
            if ti == 0:
                load_gain(layers[0], 0)
                for bl in ([0, 1], [2, 3]):
                    pre_A(xu, bl, layers[0])
                    pre_B(bl)
            if ti + 1 < n_tiles:
                load_x(ti + 1)
            for li, l in enumerate(layers):
                last = (li == len(layers) - 1)
                load_gain(l, 1)
                if not last:
                    load_gain(layers[li + 1], 0)
                    nxt = (xu, layers[li + 1])
                elif ti + 1 < n_tiles:
                    load_gain(layers[0], 0)
                    nxt = ((ti + 1) % 2, layers[0])
                else:
                    nxt = None
                if l % 2 == 0:
                    layer_A(xu, l, nxt)
                else:
                    layer_B(xu, l, ti, nxt)
            trk.op("pool", lambda g: g.dma_start(out=ot_d[ti], in_=xres[:, xu]), reads=R_x[xu], sem=f"xs{xu}", inc=16)
        for s in [f"xs{u}" for u in range(2)] + (["dbg"] if dbg else []):
            if trk.cnt.get(s, 0) > 0:
                nc.gpsimd.wait_ge(trk.sems[s], trk.cnt[s])
    return nc


def prep_inputs(inputs, b, n_tiles=8):
    f = lambda a: np.ascontiguousarray(np.asarray(a, dtype=np.float32))
    col = lambda v: f(np.asarray(v).reshape(v.shape[0], 16, 128).transpose(0, 2, 1))
    return {
        "x": f(np.asarray(inputs["x"])[b, :n_tiles * T]),
        "norm_pre": f(inputs["norm_pre"]),
        "norm_post": f(inputs["norm_post"]),
        "a_w_in": f(inputs["a_w_in"]),
        "a_ln_g": col(inputs["a_ln_g"]),
        "a_ln_b": col(inputs["a_ln_b"]),
        "a_w_s": f(np.asarray(inputs["a_w_s"]).transpose(0, 3, 1, 2)),
        "a_b_s": f(np.asarray(inputs["a_b_s"]).reshape(2, 1024)),
        "a_w_out": f(inputs["a_w_out"]),
        "b_w_in": f(inputs["b_w_in"]),
        "b_w_grp": f(inputs["b_w_grp"]),
        "b_scale": col(inputs["b_scale"]),
        "b_w_out": f(inputs["b_w_out"]),
    }


_NC_CACHE = {}


def kernel(**inputs):
    n = 8
    if "nc" not in _NC_CACHE:
        _NC_CACHE["nc"] = build_program()
    nc = _NC_CACHE["nc"]
    shared = prep_inputs(inputs, 0)
    in_maps = []
    xs = np.asarray(inputs["x"], dtype=np.float32)
    for b in range(n):
        m = dict(shared)
        m["x"] = np.ascontiguousarray(xs[b])
        in_maps.append(m)
    res = run_bass_kernel_spmd(nc, in_maps, core_ids=list(range(n)))
    out = np.stack([np.asarray(r["out"], dtype=np.float32) for r in res.results], axis=0)
    return out
```

```python
from contextlib import ExitStack

import numpy as np
import concourse.bass as bass
import concourse.mybir as mybir
from concourse.bass_utils import run_bass_kernel_spmd

F32 = mybir.dt.float32
BF16 = mybir.dt.bfloat16
AF = mybir.ActivationFunctionType
ALU = mybir.AluOpType

D = 1024
E = 2048
T = 512
NB = 4
DEPTH = 4
EPS = 1e-6
NSLOT = 5
WINDOWS = (2, 4, 8, 16)
NPA = 16
NPB = 16


class Region:
    __slots__ = ("name", "lw", "rd", "const")

    def __init__(self, name, const=False):
        self.name = name
        self.lw = None
        self.rd = []
        self.const = const


class Trk:
    def __init__(self, nc, es):
        self.nc = nc
        self.es = es
        self.eng = {"pe": nc.tensor, "act": nc.scalar, "dve": nc.vector, "pool": nc.gpsimd, "sp": nc.sync}
        self.sems = {}
        self.cnt = {}
        self.waited = {e: {} for e in self.eng}
        self.pend = {e: ([], []) for e in self.eng}
        self.esem = {}
        for e in ("pe", "act", "dve", "pool"):
            self.esem[e] = self.sem("c_" + e)

    def sem(self, name):
        if name not in self.sems:
            self.sems[name] = self.es.enter_context(self.nc.semaphore(name))
            self.cnt[name] = 0
        return name

    def op(self, e, fn, reads=(), writes=(), sig=True, sem=None, inc=1):
        deps = {}

        def add(ev):
            if ev is None:
                return
            s, v = ev
            if deps.get(s, 0) < v:
                deps[s] = v

        for r in reads:
            add(r.lw)
        for w in writes:
            add(w.lw)
            for ev in w.rd:
                add(ev)
        eng = self.eng[e]
        wd = self.waited[e]
        for s, v in deps.items():
            if wd.get(s, 0) < v:
                eng.wait_ge(self.sems[s], v)
                wd[s] = v
        inst = fn(eng)
        pr, pw = self.pend[e]
        pr.extend(r for r in reads if not r.const)
        pw.extend(writes)
        if sig:
            s = sem or self.esem[e]
            self.cnt[s] += inc
            inst.then_inc(self.sems[s], inc)
            ev = (s, self.cnt[s])
            for r in pr:
                r.rd.append(ev)
            for w in pw:
                w.lw = ev
                w.rd = []
            pr.clear()
            pw.clear()
        return inst

    def barrier(self, engines=("pe", "act", "dve", "pool", "sp")):
        for e in engines:
            for s, c in self.cnt.items():
                if c > 0 and self.waited[e].get(s, 0) < c:
                    self.eng[e].wait_ge(self.sems[s], c)
                    self.waited[e][s] = c


def handoff(group, new):
    evs = []
    for r in group:
        if r.lw is not None:
            evs.append(r.lw)
        evs.extend(r.rd)
    for r in new:
        r.rd.extend(evs)


def build_program(n_tiles=8, layers=(0, 1, 2, 3), dbg=False):
    ntok = n_tiles * T
    nc = bass.Bass("TRN2", target_bir_lowering=False)
    dt_in = lambda name, shape: nc.dram_tensor(name, shape, F32, kind="ExternalInput").ap()
    x_d = dt_in("x", [ntok, D])
    npre_d = dt_in("norm_pre", [DEPTH, D])
    npost_d = dt_in("norm_post", [DEPTH, D])
    a_w_in = dt_in("a_w_in", [2, D, 3 * E])
    a_ln_g = dt_in("a_ln_g", [2, 128, 16])
    a_ln_b = dt_in("a_ln_b", [2, 128, 16])
    a_w_s = dt_in("a_w_s", [2, 128, 8, 128])
    a_b_s = dt_in("a_b_s", [2, 1024])
    a_w_out = dt_in("a_w_out", [2, E, D])
    b_w_in = dt_in("b_w_in", [2, D, 2 * E])
    b_w_grp = dt_in("b_w_grp", [2, 4, 512, 512])
    b_scale = dt_in("b_scale", [2, 128, 16])
    b_w_out = dt_in("b_w_out", [2, E, D])
    out_d = nc.dram_tensor("out", [ntok, D], F32, kind="ExternalOutput").ap()
    scr = nc.dram_tensor("scr", [2 * NPA + 2 * NPB, 128, 4096], BF16, kind="Internal").ap()
    if dbg:
        d_hT = nc.dram_tensor("d_hT", [128, 8, T], BF16, kind="ExternalOutput").ap()
        d_vln = nc.dram_tensor("d_vln", [128, NB, E], BF16, kind="ExternalOutput").ap()
        d_yT = nc.dram_tensor("d_yT", [128, 16, T], BF16, kind="ExternalOutput").ap()
        d_ob = nc.dram_tensor("d_ob", [128, NB, D], F32, kind="ExternalOutput").ap()
        d_pl = nc.dram_tensor("d_pl", [128, 16, T], BF16, kind="ExternalOutput").ap()
        d_B = nc.dram_tensor("d_B", [128, 2, 16, 128], F32, kind="ExternalOutput").ap()
        d_tmp = nc.dram_tensor("d_tmp", [128, 2, 3, T], F32, kind="ExternalOutput").ap()
        d_g = nc.dram_tensor("d_g", [128, 2, 16], F32, kind="ExternalOutput").ap()
        d_ws = nc.dram_tensor("d_ws", [128, 2, 8, 128], BF16, kind="ExternalOutput").ap()

    def a_pieces(j):
        win = a_w_in[j].rearrange("(k p) c -> p k c", p=128)
        wout = a_w_out[j].rearrange("(k p) c -> p k c", p=128)
        P = []
        for cg in range(4):
            P.append((win[:, :, E + cg * 512:E + (cg + 1) * 512], 8))
        for jj in range(4):
            P.append((win[:, :, jj * 512:(jj + 1) * 512], 8))
            P.append((win[:, :, 2 * E + jj * 512:2 * E + (jj + 1) * 512], 8))
        for kh in range(2):
            for ch in range(2):
                P.append((wout[:, kh * 8:(kh + 1) * 8, ch * 512:(ch + 1) * 512], 8))
        return P

    def b_pieces(j):
        win = b_w_in[j].rearrange("(k p) c -> p k c", p=128)
        wg = b_w_grp[j].rearrange("g (k p) c -> p g k c", p=128)
        wout = b_w_out[j].rearrange("(k p) c -> p k c", p=128)
        P = [(win[:, :, jj * 512:(jj + 1) * 512], 8) for jj in range(4)]
        for gi in range(4):
            P.append((wg[:, gi], 4))
            P.append((win[:, :, E + gi * 512:E + (gi + 1) * 512], 8))
        for kh in range(2):
            for ch in range(2):
                P.append((wout[:, kh * 8:(kh + 1) * 8, ch * 512:(ch + 1) * 512], 8))
        return P

    layer_base = {}
    layer_pieces = {}
    base = 0
    for l in range(DEPTH):
        layer_base[l] = base
        layer_pieces[l] = a_pieces(l // 2) if l % 2 == 0 else b_pieces(l // 2)
        base += len(layer_pieces[l])

    es = ExitStack()
    with es:
        trk = Trk(nc, es)
        sb = lambda name, shape, dt: es.enter_context(nc.sbuf_tensor(name, shape, dt))

        ident = sb("ident", [128, 128], BF16)
        eps_t = sb("eps_t", [128, 1], F32)
        wsT = sb("wsT", [128, 2, 8, 128], BF16)
        Bhalf = sb("Bhalf", [128, 2, 16, 128], F32)
        ghalf = sb("ghalf", [128, 2, 16], F32)
        bscale = sb("bscale", [128, 2, 16], F32)
        invc = sb("invc", [128, 4, 16], F32)
        carry = sb("carry", [128, 2, 16, 16], F32)
        R_const = Region("const", const=True)
        R_carry = [[Region(f"carry{lb}_{c}") for c in range(16)] for lb in range(2)]

        R_scr = {}
        xres = sb("xres", [128, 2, NB, D], F32)
        hT = sb("hT", [128, 8, T], BF16)
        h_tm = sb("h_tm", [128, 2, D], BF16)
        big = sb("big", [128, 8192], F32)
        vln = sb("vln", [128, NB, E], BF16)
        yT = sb("yT", [128, 16, T], BF16)
        wring = sb("wring", [128, NSLOT, 8, 512], BF16)
        tmpA = sb("tmpA", [128, 8, T], F32)
        gains = sb("gains", [128, 2, 2, D], F32)
        junk = sb("junk", [128, D], BF16)
        stats = sb("stats", [128, NB, 4, 6], F32)
        mv = sb("mv", [128, NB, 2], F32)
        nrm = sb("nrm", [128, 8, NB], F32)
        t16 = sb("t16", [128, 2, 16], F32)
        sso = sb("sso", [128, 2, NB], F32)
        banks = [es.enter_context(nc.psum_tensor(f"bank{i}", [128, 512], F32)) for i in range(8)]

        vbuf = big[:].rearrange("p (b e) -> p b e", b=NB)
        obuf = big[:, 0:4096].rearrange("p (b d) -> p b d", b=NB)
        pooledT = big[:, 0:4096].bitcast(BF16).rearrange("p (c t) -> p c t", c=16)
        PBW = 528
        pbv = big[:, 4096:4096 + 7 * PBW].rearrange("p (i w) -> p i w", i=7)

        R_x = [[Region(f"x{u}_{b}") for b in range(NB)] for u in range(2)]
        R_hT = [Region(f"hT{b}") for b in range(NB)]
        R_htm = [Region(f"htm{i}") for i in range(2)]
        R_vbuf = [[Region(f"vbuf{b}_{cg}") for cg in range(4)] for b in range(NB)]
        R_obuf = [[Region(f"obuf{b}_{ch}") for ch in range(2)] for b in range(NB)]
        R_pooled = [Region(f"pooled{c}") for c in range(16)]
        R_pb = [[Region(f"pb{s}_{i}") for i in range(2)] for s in range(3)] + \
               [[Region(f"pbt{e}_{i}") for i in range(2)] for e in range(2)]
        big_group = [r for row in R_vbuf for r in row] + [r for row in R_obuf for r in row] + R_pooled + \
                    [r for row in R_pb for r in row]
        R_vln = [Region(f"vln{b}") for b in range(NB)]
        R_yT = [Region(f"yT{c}") for c in range(16)]
        R_slot = [Region(f"slot{s}") for s in range(NSLOT)]
        R_tmp = [Region(f"tmp{i}") for i in range(8)]
        R_gain = [[Region(f"gain{u}_{i}") for i in range(2)] for u in range(2)]
        R_junk = Region("junk")
        R_stats = [[Region(f"st{b}_{cg}") for cg in range(4)] for b in range(NB)]
        R_mv = [Region(f"mv{b}") for b in range(NB)]
        R_nrm = [[Region(f"nrm{i}_{b}") for b in range(NB)] for i in range(8)]
        R_t16 = [Region("t16_0"), Region("t16_1")]
        R_sso = [[Region(f"sso{ch}_{b}") for b in range(NB)] for ch in range(2)]
        R_bank = [Region(f"bank{i}") for i in range(8)]
        for s in range(NSLOT):
            trk.sem(f"w{s}")
        for u in range(2):
            trk.sem(f"xl{u}")
            trk.sem(f"xs{u}")
            trk.sem(f"gl{u}0")
            trk.sem(f"gl{u}1")

        bank_free = list(range(8))

        def next_bank():
            assert bank_free, "out of PSUM banks"
            return bank_free.pop(0)

        def free_bank(i):
            assert i not in bank_free
            bank_free.append(i)

        wseq = []
        for ti in range(n_tiles):
            g_ = 0
            for l in layers:
                for i in range(len(layer_pieces[l])):
                    wseq.append((ti, layer_base[l] + i, layer_pieces[l][i][1], layer_pieces[l][i][0], g_))
                    g_ += 1
        n_pt = len(wseq) // n_tiles
        wst = {"load": 0, "use": 0, "done": 0, "conv": 0}
        R_scrp = {}
        KC = 20
        LOOK = 16
        for k_ in range(KC):
            trk.sem(f"cv{k_}")

        def conv_upto(n):
            while wst["conv"] < min(n, n_pt):
                g_ = wst["conv"]
                _, si, nk, src, _ = wseq[g_]
                sname = f"cv{g_ % KC}"
                dst = scr[si][:, 0:nk * 512].rearrange("p (k c) -> p k c", c=512)
                nc.gpsimd.dma_start(out=dst, in_=src).then_inc(trk.sems[sname], 16)
                trk.cnt[sname] += 16
                R_scrp[si] = Region(f"scrp{si}")
                R_scrp[si].lw = (sname, trk.cnt[sname])
                wst["conv"] += 1

        def w_prefetch():
            while wst["load"] < len(wseq) and wst["load"] < wst["done"] + NSLOT:
                i = wst["load"]
                s = i % NSLOT
                ti_, si, nk, src, g_ = wseq[i]
                if ti_ == 0:
                    conv_upto(g_ + 1)
                trk.op("sp", lambda g, s=s, si=si, nk=nk: g.dma_start(
                    out=wring[:, s, 0:nk].rearrange("p k c -> p (k c)"), in_=scr[si][:, 0:nk * 512]),
                    reads=[R_scrp[si]], writes=[R_slot[s]], sem=f"w{s}", inc=16)
                wst["load"] += 1

        def w_get():
            w_prefetch()
            assert wst["load"] > wst["use"], "weight ring deadlock"
            s = wst["use"] % NSLOT
            wst["use"] += 1
            return s

        def w_done(n=1):
            wst["done"] += n
            if wst["done"] <= n_pt:
                conv_upto(wst["done"] + LOOK)
            w_prefetch()

        def rsqrt_into(dst_ap, dst_reg, src_ap, src_reg, scale):
            trk.op("act", lambda g: g.activation(out=dst_ap, in_=src_ap, func=AF.Sqrt, bias=eps_t[:, 0:1], scale=scale),
                   reads=[src_reg, R_const], writes=[dst_reg])
            trk.op("dve", lambda g: g.reciprocal(out=dst_ap, in_=dst_ap), reads=[dst_reg], writes=[dst_reg])

        def load_gain(l, which):
            u = l % 2
            src = npre_d if which == 0 else npost_d
            trk.op("sp", lambda g: g.dma_start(out=gains[:, u, which, :], in_=src[l].partition_broadcast(128)),
                   writes=[R_gain[u][which]], sem=f"gl{u}{which}", inc=16)

        def rsqrt_cols(dst_row, src_row, b0, b1, scale):
            regs_s = [R_nrm[src_row][b] for b in range(b0, b1)]
            regs_d = [R_nrm[dst_row][b] for b in range(b0, b1)]
            trk.op("act", lambda g: g.activation(out=nrm[:, dst_row, b0:b1], in_=nrm[:, src_row, b0:b1], func=AF.Sqrt,
                                                 bias=eps_t[:, 0:1], scale=scale),
                   reads=regs_s + [R_const], writes=regs_d)
            trk.op("dve", lambda g: g.reciprocal(out=nrm[:, dst_row, b0:b1], in_=nrm[:, dst_row, b0:b1]),
                   reads=regs_d, writes=regs_d)

        pending_late = []

        def run_late():
            while pending_late:
                pending_late.pop(0)()

        def pre_A(xu, bl, l):
            u = l % 2
            for b in bl:
                xv = xres[:, xu, b, :]
                trk.op("act", lambda g, b=b, xv=xv: g.activation(out=junk[:], in_=xv, func=AF.Square, accum_out=nrm[:, 2, b:b + 1]),
                       reads=[R_x[xu][b]], writes=[R_nrm[2][b]])
            for b in bl:
                rsqrt_cols(3, 2, b, b + 1, 1.0 / D)
            for b in bl:
                hb = b % 2
                xv = xres[:, xu, b, :]
                trk.op("dve", lambda g, b=b, xv=xv, hb=hb: g.scalar_tensor_tensor(out=h_tm[:, hb, :], in0=xv, scalar=nrm[:, 3, b:b + 1],
                                                                                  in1=gains[:, u, 0, :], op0=ALU.mult, op1=ALU.mult),
                       reads=[R_x[xu][b], R_nrm[3][b], R_gain[u][0]], writes=[R_htm[hb]])

        def pre_B(bl):
            for b in bl:
                hb = b % 2
                bi = next_bank()
                tp = banks[bi][:].bitcast(BF16).rearrange("p (k t) -> p k t", t=128)
                for k in range(8):
                    trk.op("pe", lambda g, k=k, tp=tp, hb=hb: g.transpose(tp[:, k, :], h_tm[:, hb, k * 128:(k + 1) * 128], ident[:]),
                           reads=[R_htm[hb], R_const], writes=[R_bank[bi]], sig=(k == 7))
                if b % 2 == 0:
                    trk.op("act", lambda g, b=b, tp=tp: g.activation(out=hT[:, :, b * 128:(b + 1) * 128], in_=tp, func=AF.Copy),
                           reads=[R_bank[bi]], writes=[R_hT[b]])
                else:
                    trk.op("dve", lambda g, b=b, tp=tp: g.tensor_copy(out=hT[:, :, b * 128:(b + 1) * 128], in_=tp),
                           reads=[R_bank[bi]], writes=[R_hT[b]])
                free_bank(bi)

        def out_proj_and_norm(xu, l, yreg_all, nxt):
            u = l % 2
            handoff(big_group, [r for row in R_obuf for r in row])
            if dbg:
                trk.sem("dbg")
                trk.op("pool", lambda g: g.dma_start(out=d_yT, in_=yT[:]), reads=R_yT, sem="dbg", inc=16)
            ss = [w_get() for _ in range(4)]
            groups = [[0, 1], [2, 3]]
            bis = [[None] * NB for _ in range(2)]

            def mm(half, release):
                for ch in range(2):
                    for b in half:
                        bis[ch][b] = next_bank()
                for kh in range(2):
                    for ch in range(2):
                        s = ss[kh * 2 + ch]
                        for b in half:
                            for k in range(8):
                                first = (kh == 0 and k == 0)
                                last = (kh == 1 and k == 7)
                                trk.op("pe", lambda g, b=b, k=k, s=s, kh=kh, ch=ch, first=first, last=last: g.matmul(
                                    banks[bis[ch][b]][:], lhsT=yT[:, kh * 8 + k, b * 128:(b + 1) * 128], rhs=wring[:, s, k, :],
                                    start=first, stop=last),
                                    reads=([yreg_all[kh * 8 + kk] for kk in range(8)] + [R_slot[s]]) if k == 0 else [R_slot[s]], writes=[R_bank[bis[ch][b]]],
                                    sig=(last or (b == half[-1] and k == 7)))
                        if release:
                            w_done()

            def post(bl):
                b0, b1 = bl[0], bl[-1] + 1
                for ch in range(2):
                    for b in bl:
                        trk.op("act", lambda g, b=b, ch=ch: g.activation(out=junk[:, 0:512], in_=banks[bis[ch][b]][:], func=AF.Square,
                                                                          accum_out=sso[:, ch, b:b + 1]),
                               reads=[R_bank[bis[ch][b]]], writes=[R_sso[ch][b]])
                trk.op("dve", lambda g: g.tensor_tensor(out=nrm[:, 0, b0:b1], in0=sso[:, 0, b0:b1], in1=sso[:, 1, b0:b1], op=ALU.add),
                       reads=[R_sso[c][b] for c in range(2) for b in bl], writes=[R_nrm[0][b] for b in bl])
                rsqrt_cols(1, 0, b0, b1, 1.0 / D)
                for b in bl:
                    xv = xres[:, xu, b, :]
                    for ch in range(2):
                        trk.op("dve", lambda g, b=b, ch=ch: g.scalar_tensor_tensor(
                            out=obuf[:, b, ch * 512:(ch + 1) * 512], in0=banks[bis[ch][b]][:], scalar=nrm[:, 1, b:b + 1],
                            in1=gains[:, u, 1, ch * 512:(ch + 1) * 512], op0=ALU.mult, op1=ALU.mult),
                            reads=[R_bank[bis[ch][b]], R_nrm[1][b], R_gain[u][1]], writes=[R_obuf[b][ch]])
                    free_bank(bis[0][b])
                    free_bank(bis[1][b])
                    trk.op("dve", lambda g, b=b, xv=xv: g.tensor_tensor(out=xv, in0=obuf[:, b, :], in1=xv, op=ALU.add),
                           reads=R_obuf[b] + [R_x[xu][b]], writes=[R_x[xu][b]])

            for gi, g in enumerate(groups):
                mm(g, gi == len(groups) - 1)
                if gi > 0 and nxt is not None:
                    pre_B(groups[gi - 1])
                post(g)
                if nxt is not None:
                    pre_A(nxt[0], g, nxt[1])
            if nxt is not None:
                pending_late.append(lambda: pre_B(groups[-1]))

        def ln_all():
            for b in range(NB):
                trk.op("dve", lambda g, b=b: g.bn_aggr(out=mv[:, b, :], in_=stats[:, b].rearrange("p c s -> p (c s)")),
                       reads=R_stats[b], writes=[R_mv[b]])
            trk.op("act", lambda g: g.activation(out=nrm[:, 4, :], in_=mv[:, :, 1], func=AF.Sqrt, bias=eps_t[:, 0:1], scale=1.0),
                   reads=R_mv + [R_const], writes=R_nrm[4])
            trk.op("dve", lambda g: g.reciprocal(out=nrm[:, 4, :], in_=nrm[:, 4, :]), reads=R_nrm[4], writes=R_nrm[4])
            trk.op("dve", lambda g: g.scalar_tensor_tensor(out=nrm[:, 5, :], in0=mv[:, :, 0], scalar=-1.0, in1=nrm[:, 4, :],
                                                           op0=ALU.mult, op1=ALU.mult),
                   reads=R_mv + R_nrm[4], writes=R_nrm[5])
            for b in range(NB):
                if b % 2 == 0:
                    trk.op("dve", lambda g, b=b: g.tensor_scalar(out=vln[:, b, :], in0=vbuf[:, b, :], scalar1=mv[:, b, 0:1],
                                                                 scalar2=nrm[:, 4, b:b + 1], op0=ALU.subtract, op1=ALU.mult),
                           reads=R_vbuf[b] + [R_mv[b], R_nrm[4][b]], writes=[R_vln[b]])
                else:
                    trk.op("act", lambda g, b=b: g.activation(out=vln[:, b, :], in_=vbuf[:, b, :], func=AF.Identity,
                                                              bias=nrm[:, 5, b:b + 1], scale=nrm[:, 4, b:b + 1]),
                           reads=R_vbuf[b] + [R_nrm[5][b], R_nrm[4][b]], writes=[R_vln[b]])

        def layer_A(xu, l, nxt):
            j = l // 2
            handoff(big_group, [r for row in R_vbuf for r in row])
            sv4 = [w_get() for _ in range(4)]

            def a1_job(cg, b):
                s = sv4[cg]
                bi = next_bank()
                for k in range(8):
                    trk.op("pe", lambda g, k=k: g.matmul(banks[bi][:], lhsT=hT[:, k, b * 128:(b + 1) * 128], rhs=wring[:, s, k, :],
                                                         start=(k == 0), stop=(k == 7)),
                           reads=[R_hT[b], R_slot[s]], writes=[R_bank[bi]], sig=(k == 7))
                vv = vbuf[:, b, cg * 512:(cg + 1) * 512]
                trk.op("act", lambda g: g.activation(out=vv, in_=banks[bi][:], func=AF.Gelu_apprx_tanh),
                       reads=[R_bank[bi]], writes=[R_vbuf[b][cg]])
                free_bank(bi)
                trk.op("dve", lambda g: g.bn_stats(out=stats[:, b, cg, :], in_=vv),
                       reads=[R_vbuf[b][cg]], writes=[R_stats[b][cg]])

            for cg in range(4):
                for b in (0, 1):
                    a1_job(cg, b)
            run_late()
            for cg in range(4):
                for b in (2, 3):
                    a1_job(cg, b)
                w_done()
            ln_all()
            if dbg:
                trk.sem("dbg")
                trk.op("pool", lambda g: g.dma_start(out=d_hT, in_=hT[:]), reads=R_hT, sem="dbg", inc=16)
                trk.op("pool", lambda g: g.dma_start(out=d_vln, in_=vln[:]), reads=R_vln, sem="dbg", inc=16)
            LAG = 3
            pend = []

            def finish(c):
                gg = c // 2
                ub = c % 4
                tb = c % 2
                bm = next_bank()
                for b in range(NB):
                    trk.op("pe", lambda g, b=b: g.matmul(banks[bm][:, b * 128:(b + 1) * 128],
                                                         lhsT=vln[:, b, c * 128:(c + 1) * 128],
                                                         rhs=wsT[:, j, gg, :], start=True, stop=True),
                           reads=[R_vln[b], R_const], writes=[R_bank[bm]], sig=(b == NB - 1))
                t1 = tmpA[:, 6 + tb, :]
                uv = tmpA[:, ub, :]
                trk.op("dve", lambda g: g.scalar_tensor_tensor(
                    out=t1.rearrange("p (b t) -> p b t", t=128),
                    in0=banks[bm][:].rearrange("p (b t) -> p b t", t=128),
                    scalar=ghalf[:, j, c:c + 1],
                    in1=Bhalf[:, j, c:c + 1, :].to_broadcast([128, NB, 128]),
                    op0=ALU.mult, op1=ALU.add),
                    reads=[R_bank[bm], R_const], writes=[R_tmp[6 + tb]])
                free_bank(bm)
                trk.op("dve", lambda g: g.tensor_tensor(out=yT[:, c, :], in0=t1, in1=uv, op=ALU.mult),
                       reads=[R_tmp[6 + tb], R_tmp[ub]], writes=[R_yT[c]])

            for jj in range(4):
                su = w_get()
                sz = w_get()
                for cc in range(4):
                    c = jj * 4 + cc
                    ub = c % 4
                    tb = c % 2
                    bu, bz = next_bank(), next_bank()
                    for k in range(8):
                        trk.op("pe", lambda g, k=k, cc=cc: g.matmul(banks[bu][:], lhsT=wring[:, su, k, cc * 128:(cc + 1) * 128],
                                                                    rhs=hT[:, k, :], start=(k == 0), stop=(k == 7)),
                               reads=R_hT + [R_slot[su]], writes=[R_bank[bu]], sig=(k == 7))
                    for k in range(8):
                        trk.op("pe", lambda g, k=k, cc=cc: g.matmul(banks[bz][:], lhsT=wring[:, sz, k, cc * 128:(cc + 1) * 128],
                                                                    rhs=hT[:, k, :], start=(k == 0), stop=(k == 7)),
                               reads=R_hT + [R_slot[sz]], writes=[R_bank[bz]], sig=(k == 7))
                    uv, sv = tmpA[:, ub, :], tmpA[:, 4 + tb, :]
                    trk.op("act", lambda g, uv=uv: g.activation(out=uv, in_=banks[bu][:], func=AF.Gelu_apprx_tanh),
                           reads=[R_bank[bu]], writes=[R_tmp[ub]])
                    free_bank(bu)
                    trk.op("act", lambda g, sv=sv: g.activation(out=sv, in_=banks[bz][:], func=AF.Tanh, scale=0.5),
                           reads=[R_bank[bz]], writes=[R_tmp[4 + tb]])
                    trk.op("dve", lambda g, sv=sv: g.scalar_tensor_tensor(out=sv, in0=sv, scalar=1.0, in1=banks[bz][:],
                                                                         op0=ALU.add, op1=ALU.mult),
                           reads=[R_tmp[4 + tb], R_bank[bz]], writes=[R_tmp[4 + tb]])
                    free_bank(bz)
                    trk.op("pool", lambda g, uv=uv, sv=sv: g.tensor_tensor(out=uv, in0=uv, in1=sv, op=ALU.mult),
                           reads=[R_tmp[ub], R_tmp[4 + tb]], writes=[R_tmp[ub]])
                    pend.append(c)
                    if len(pend) > LAG:
                        finish(pend.pop(0))
                w_done(2)
            while pend:
                finish(pend.pop(0))
            out_proj_and_norm(xu, l, R_yT, nxt)

        def layer_B(xu, l, ti, nxt):
            j = l // 2
            lb = j
            handoff(big_group, R_pooled + [r for row in R_pb for r in row])
            def b1_mm(s, cc, bi, lo, hi):
                for k in range(8):
                    trk.op("pe", lambda g, k=k: g.matmul(banks[bi][:, lo:hi], lhsT=wring[:, s, k, cc * 128:(cc + 1) * 128],
                                                         rhs=hT[:, k, lo:hi], start=(k == 0), stop=(k == 7)),
                           reads=[R_hT[bb] for bb in range(lo // 128, hi // 128)] + [R_slot[s]], writes=[R_bank[bi]],
                           sig=(k == 7))

            def b1_post(c, bi):
                gi = c // 4
                w = WINDOWS[gi]
                si = c % 3
                ve = "pool" if (c % 4 == 3 or (c % 4 == 1 and gi in (1, 2))) else "dve"
                ei = 1 if ve == "pool" else 0
                p0, pA, pB = pbv[:, si, :], pbv[:, 3 + 2 * ei, :], pbv[:, 4 + 2 * ei, :]
                Rh, Rm = R_pb[si]
                RA, RB = R_pb[3 + ei]
                if ti == 0:
                    trk.op("act", lambda g: g.activation(out=p0[:, 0:16], in_=banks[bi][:, 0:16], func=AF.Copy, scale=0.0),
                           reads=[R_bank[bi]], writes=[Rh])
                else:
                    trk.op("act", lambda g: g.activation(out=p0[:, 0:16], in_=carry[:, lb, c, :], func=AF.Copy),
                           reads=[R_carry[lb][c]], writes=[Rh])
                trk.op("act", lambda g: g.activation(out=carry[:, lb, c, :], in_=banks[bi][:, T - 16:T], func=AF.Copy),
                       reads=[R_bank[bi]], writes=[R_carry[lb][c]])
                trk.op("act", lambda g: g.activation(out=p0[:, 16:PBW], in_=banks[bi][:], func=AF.Copy),
                       reads=[R_bank[bi]], writes=[Rm])
                free_bank(bi)
                steps = [(p0, pA, 1, [Rh, Rm], RA), (pA, pB, 2, [RA], RB), (pB, pA, 4, [RB], RA), (pA, pB, 8, [RA], RB)]
                nst = {2: 1, 4: 2, 8: 3, 16: 4}[w]
                lo = 0
                for (src, dst, sh, rr, wr) in steps[:nst]:
                    lo += sh
                    trk.op(ve, lambda g, src=src, dst=dst, sh=sh, lo=lo: g.tensor_tensor(
                        out=dst[:, lo:PBW], in0=src[:, lo:PBW], in1=src[:, lo - sh:PBW - sh], op=ALU.add),
                        reads=rr, writes=[wr])
                fin, Rf = steps[nst - 1][1], steps[nst - 1][4]
                trk.op("dve", lambda g: g.scalar_tensor_tensor(
                    out=pooledT[:, c, :], in0=fin[:, 16:PBW], scalar=1.0 / w, in1=p0[:, 16:PBW],
                    op0=ALU.mult, op1=ALU.subtract),
                    reads=[Rf, Rm], writes=[R_pooled[c]])
                if ti == 0:
                    tt = t16[:, ei, :]
                    trk.op(ve, lambda g: g.tensor_tensor(out=tt, in0=fin[:, 16:32], in1=invc[:, gi, :], op=ALU.mult),
                           reads=[Rf, R_const], writes=[R_t16[ei]])
                    trk.op(ve, lambda g: g.tensor_tensor(out=pooledT[:, c, 0:16], in0=tt, in1=p0[:, 16:32], op=ALU.subtract),
                           reads=[R_t16[ei], Rm, R_pooled[c]], writes=[R_pooled[c]])

            run_late()
            for jj in range(4):
                s = w_get()
                for cc in range(4):
                    bi = next_bank()
                    b1_mm(s, cc, bi, 0, 512)
                    b1_post(jj * 4 + cc, bi)
                w_done()
            if dbg:
                trk.sem("dbg")
                trk.op("pool", lambda g: g.dma_start(out=d_hT, in_=hT[:]), reads=R_hT, sem="dbg", inc=16)
                trk.op("pool", lambda g: g.dma_start(out=d_pl, in_=pooledT), reads=R_pooled, sem="dbg", inc=16)
            sg = None
            szs = None
            for c in range(16):
                gi = c // 4
                cc = c % 4
                ts = c % 2
                if c % 4 == 0:
                    sg = w_get()
                    szs = w_get()
                bm, bz = next_bank(), next_bank()
                for k in range(4):
                    trk.op("pe", lambda g, k=k, gi=gi, cc=cc: g.matmul(
                        banks[bm][:], lhsT=wring[:, sg, k, cc * 128:(cc + 1) * 128],
                        rhs=pooledT[:, gi * 4 + k, :], start=(k == 0), stop=(k == 3)),
                        reads=[R_pooled[gi * 4 + k], R_slot[sg]], writes=[R_bank[bm]], sig=(k == 3))
                for k in range(8):
                    trk.op("pe", lambda g, k=k, cc=cc: g.matmul(banks[bz][:], lhsT=wring[:, szs, k, cc * 128:(cc + 1) * 128],
                                                                rhs=hT[:, k, :], start=(k == 0), stop=(k == 7)),
                           reads=R_hT + [R_slot[szs]], writes=[R_bank[bz]], sig=(k == 7))
                zv = tmpA[:, ts, :]
                mvv = tmpA[:, 2 + ts, :]
                trk.op("act", lambda g, zv=zv: g.activation(out=zv, in_=banks[bz][:], func=AF.Silu),
                       reads=[R_bank[bz]], writes=[R_tmp[ts]])
                free_bank(bz)
                trk.op("act", lambda g, c=c, mvv=mvv: g.activation(out=mvv, in_=banks[bm][:], func=AF.Identity, scale=bscale[:, j, c:c + 1]),
                       reads=[R_bank[bm], R_const], writes=[R_tmp[2 + ts]])
                trk.op("pool", lambda g, c=c, zv=zv, mvv=mvv: g.tensor_tensor(out=yT[:, c, :], in0=mvv, in1=zv, op=ALU.mult),
                       reads=[R_tmp[2 + ts], R_tmp[ts]], writes=[R_yT[c]])
                free_bank(bm)
                if c % 4 == 3:
                    w_done(2)
            out_proj_and_norm(xu, l, R_yT, nxt)

        xt_d = x_d.rearrange("(n b p) d -> n p b d", b=NB, p=128)
        ot_d = out_d.rearrange("(n b p) d -> n p b d", b=NB, p=128)

        def load_x(ti):
            u = ti % 2
            trk.op("pool", lambda g: g.dma_start(out=xres[:, u], in_=xt_d[ti]), writes=R_x[u], sem=f"xl{u}", inc=16)

        if dbg:
            trk.sem("dbg")
            trk.op("pool", lambda g: g.dma_start(out=d_B, in_=Bhalf[:]), sem="dbg", inc=16)
            trk.op("pool", lambda g: g.dma_start(out=d_g, in_=ghalf[:]), sem="dbg", inc=16)
            trk.op("pool", lambda g: g.dma_start(out=d_ws, in_=wsT[:]), sem="dbg", inc=16)
        load_x(0)
        conv_upto(4)
        for k_ in range(4):
            nc.gpsimd.wait_ge(trk.sems[f"cv{k_}"], trk.cnt[f"cv{k_}"])
        conv_upto(LOOK)
        w_prefetch()
        if True:
            ws32 = big[:, 0:1024].rearrange("p (g q) -> p g q", g=8)
            bsbc = big[:, 1024:2048].rearrange("p (g q) -> p g q", g=8)
            ones32 = big[:, 2048:2176]
            gcol = big[:, 2176:2208].rearrange("p (j c) -> p j c", j=2)
            bcol = big[:, 2208:2240].rearrange("p (j c) -> p j c", j=2)
            rs_ps = [banks[0], banks[1]]
            R_ws32, R_ones, R_bsbc, R_gcol, R_bcol = (Region(n) for n in ("ws32", "ones", "bsbc", "gcol", "bcol"))
            R_rs = [Region("rs0"), Region("rs1")]
            R_c = {n: Region(n) for n in ("ident", "eps", "wsT", "Bhalf", "ghalf", "bscale", "invc")}
            ld = lambda out, in_, w: trk.op("sp", lambda g: g.dma_start(out=out, in_=in_), writes=[w],
                                            sem=trk.sem("sl_" + w.name), inc=16)

            trk.op("pool", lambda g: g.memset(ident[:], 0.0), writes=[R_c["ident"]])
            trk.op("pool", lambda g: g.affine_select(out=ident[:], in_=ident[:], compare_op=ALU.not_equal, fill=1.0,
                                                      base=0, pattern=[[-1, 128]], channel_multiplier=1),
                   reads=[R_c["ident"]], writes=[R_c["ident"]])
            trk.op("dve", lambda g: g.memset(eps_t[:], EPS), writes=[R_c["eps"]])
            trk.op("dve", lambda g: g.memset(ones32, 1.0), writes=[R_ones])
            for gi, w in enumerate(WINDOWS):
                trk.op("dve", lambda g, gi=gi, w=w: g.memset(invc[:, gi, :], 1.0 / w), writes=[R_c["invc"]])
                for t in range(w - 1):
                    trk.op("dve", lambda g, gi=gi, t=t: g.memset(invc[:, gi, t:t + 1], 1.0 / (t + 1)), writes=[R_c["invc"]])
            ld(gcol, a_ln_g.rearrange("j p c -> p j c"), R_gcol)
            ld(bcol, a_ln_b.rearrange("j p c -> p j c"), R_bcol)
            ld(bscale[:], b_scale.rearrange("j p c -> p j c"), R_c["bscale"])
            trk.op("dve", lambda g: g.tensor_scalar(out=ghalf[:], in0=gcol, scalar1=0.5, scalar2=None, op0=ALU.mult),
                   reads=[R_gcol], writes=[R_c["ghalf"]])
            for j in range(2):
                ld(ws32, a_w_s[j], R_ws32)
                ld(bsbc.rearrange("p g q -> p (g q)"), a_b_s[j].partition_broadcast(128), R_bsbc)
                trk.op("dve", lambda g: g.memset(ws32[64:128, :, 0:64], 0.0), reads=[R_ws32], writes=[R_ws32])
                trk.op("dve", lambda g, j=j: g.tensor_copy(out=wsT[:, j], in_=ws32), reads=[R_ws32], writes=[R_c["wsT"]])
                for hh in range(2):
                    trk.op("pe", lambda g, hh=hh: g.matmul(rs_ps[hh][:], lhsT=ones32,
                                                          rhs=ws32[:, hh * 4:(hh + 1) * 4, :].rearrange("p g q -> p (g q)"),
                                                          start=True, stop=True),
                           reads=[R_ones, R_ws32], writes=[R_rs[hh]])
                for cc in range(16):
                    gg = cc // 2
                    rsv = rs_ps[gg // 4][:, (gg % 4) * 128:(gg % 4 + 1) * 128]
                    trk.op("dve", lambda g, j=j, cc=cc, gg=gg, rsv=rsv: g.scalar_tensor_tensor(
                        out=Bhalf[:, j, cc, :], in0=rsv, scalar=bcol[:, j, cc:cc + 1], in1=bsbc[:, gg, :],
                        op0=ALU.mult, op1=ALU.add),
                        reads=[R_rs[gg // 4], R_bcol, R_bsbc], writes=[R_c["Bhalf"]])
                trk.op("dve", lambda g, j=j: g.tensor_scalar(out=Bhalf[:, j], in0=Bhalf[:, j], scalar1=0.5, scalar2=None,
                                                            op0=ALU.mult),
                       reads=[R_c["Bhalf"]], writes=[R_c["Bhalf"]])
            trk.barrier()

        for ti in range(n_tiles):
            xu = ti % 2
            if ti == 0:
                load_gain(layers[0], 0)
                for bl in ([0, 1], [2, 3]):
                    pre_A(xu, bl, layers[0])
                    pre_B(bl)
            if ti + 1 < n_tiles:
                load_x(ti + 1)
            for li, l in enumerate(layers):
                last = (li == len(layers) - 1)
                load_gain(l, 1)
                if not last:
                    load_gain(layers[li + 1], 0)
                    nxt = (xu, layers[li + 1])
                elif ti + 1 < n_tiles:
                    load_gain(layers[0], 0)
                    nxt = ((ti + 1) % 2, layers[0])
                else:
                    nxt = None
                if l % 2 == 0:
                    layer_A(xu, l, nxt)
                else:
                    layer_B(xu, l, ti, nxt)
            trk.op("pool", lambda g: g.dma_start(out=ot_d[ti], in_=xres[:, xu]), reads=R_x[xu], sem=f"xs{xu}", inc=16)
        for s in [f"xs{u}" for u in range(2)] + (["dbg"] if dbg else []):
            if trk.cnt.get(s, 0) > 0:
                nc.gpsimd.wait_ge(trk.sems[s], trk.cnt[s])
    return nc


def prep_inputs(inputs, b, n_tiles=8):
    f = lambda a: np.ascontiguousarray(np.asarray(a, dtype=np.float32))
    col = lambda v: f(np.asarray(v).reshape(v.shape[0], 16, 128).transpose(0, 2, 1))
    return {
        "x": f(np.asarray(inputs["x"])[b, :n_tiles * T]),
        "norm_pre": f(inputs["norm_pre"]),
        "norm_post": f(inputs["norm_post"]),
        "a_w_in": f(inputs["a_w_in"]),
        "a_ln_g": col(inputs["a_ln_g"]),
        "a_ln_b": col(inputs["a_ln_b"]),
        "a_w_s": f(np.asarray(inputs["a_w_s"]).transpose(0, 3, 1, 2)),
        "a_b_s": f(np.asarray(inputs["a_b_s"]).reshape(2, 1024)),
        "a_w_out": f(inputs["a_w_out"]),
        "b_w_in": f(inputs["b_w_in"]),
        "b_w_grp": f(inputs["b_w_grp"]),
        "b_scale": col(inputs["b_scale"]),
        "b_w_out": f(inputs["b_w_out"]),
    }


_NC_CACHE = {}


def kernel(**inputs):
    n = 8
    if "nc" not in _NC_CACHE:
        _NC_CACHE["nc"] = build_program()
    nc = _NC_CACHE["nc"]
    shared = prep_inputs(inputs, 0)
    in_maps = []
    xs = np.asarray(inputs["x"], dtype=np.float32)
    for b in range(n):
        m = dict(shared)
        m["x"] = np.ascontiguousarray(xs[b])
        in_maps.append(m)
    res = run_bass_kernel_spmd(nc, in_maps, core_ids=list(range(n)))
    out = np.stack([np.asarray(r["out"], dtype=np.float32) for r in res.results], axis=0)
    return out
```
